# Optimizing a Trainium2 kernel written in Bass

```python
import math
import jax, jax.numpy as jnp
from jax import lax
import numpy as np

D_MODEL = 2048
BATCH = 16
SEQ = 256
DEPTH = 4
DEC_BATCH = 2
DEC_SEQ = 2048
PAST_LEN = 512

GRID_W = 64
N_MIXERS = 2
N_RG_LAYERS = (DEPTH + 1) // 2
N_ATTN_LAYERS = DEPTH // 2
N_HEADS = 16
N_KV_HEADS = 4
HEAD_DIM = D_MODEL // N_HEADS
ROT_PAIRS = HEAD_DIM // 4
ROPE_THETA = 10000.0
Q_BLOCK = 128
RNN_WIDTH = D_MODEL
RG_BLOCKS = 16
RG_BLOCK_SIZE = RNN_WIDTH // RG_BLOCKS
RG_C = 8.0
CONV_W = 4
CONV_LEFT = 1
D_FF = 4 * D_MODEL
EPS = 1e-6

kernel_name = "hybrid_rglru_gqa_diffusion_step"


def rmsnorm(x, g):
    xf = x.astype(jnp.float32)
    y = xf * lax.rsqrt(jnp.mean(xf * xf, axis=-1, keepdims=True) + EPS)
    return (y * g.astype(jnp.float32)).astype(x.dtype)


def modulation(cvec, w_mod_l, b_mod_l):
    m = jax.nn.silu(cvec) @ w_mod_l + b_mod_l
    return [t[:, None, :] for t in jnp.split(m, 6, axis=-1)]


def depthwise_conv(x, w, b):
    T = x.shape[1]
    xp = jnp.pad(x, ((0, 0), (CONV_LEFT, CONV_W - 1 - CONV_LEFT), (0, 0)))
    y = b + xp[:, 0:T] * w[0]
    for k in range(1, CONV_W):
        y = y + xp[:, k:k + T] * w[k]
    return y


def rglru_mixer(h, h0, w_in, conv_w, conv_b, w_a, b_a, w_x, b_x, lam, w_out):
    B, T, _ = h.shape
    proj = h @ w_in
    gate_branch, xb = jnp.split(proj, 2, axis=-1)
    xb = depthwise_conv(xb, conv_w, conv_b)
    xblk = xb.reshape(B, T, RG_BLOCKS, RG_BLOCK_SIZE)
    r = jax.nn.sigmoid(jnp.einsum('btnj,enjk->betnk', xblk, w_a).reshape(B, 2, T, RNN_WIDTH)
                       + b_a[None, :, None, :]).astype(jnp.float32)
    i = jax.nn.sigmoid(jnp.einsum('btnj,enjk->betnk', xblk, w_x).reshape(B, 2, T, RNN_WIDTH)
                       + b_x[None, :, None, :]).astype(jnp.float32)
    log_a = -RG_C * jax.nn.softplus(-lam.astype(jnp.float32))[None, :, None, :] * r
    a = jnp.exp(log_a)
    u = jnp.sqrt(-jnp.expm1(2.0 * log_a)) * i * xb.astype(jnp.float32)[:, None]
    a = jnp.stack([a[:, 0], jnp.flip(a[:, 1], axis=1)], axis=1)
    u = jnp.stack([u[:, 0], jnp.flip(u[:, 1], axis=1)], axis=1)
    a_t = jnp.moveaxis(a, 2, 0)
    u_t = jnp.moveaxis(u, 2, 0)

    def step(hc, inp):
        at, ut = inp
        hn = at * hc + ut
        return hn, hn

    h_final, hs = lax.scan(step, h0.astype(jnp.float32), (a_t, u_t))
    y = hs[:, :, 0] + jnp.flip(hs[:, :, 1], axis=0)
    y = jnp.moveaxis(y, 0, 1).astype(h.dtype)
    out = (y * jax.nn.gelu(gate_branch)) @ w_out
    return out, h_final.astype(h.dtype)


def qkv_heads(h, w_qkv, g_q, g_k):
    B, T, _ = h.shape
    qkv = h @ w_qkv
    q = qkv[..., :N_HEADS * HEAD_DIM].reshape(B, T, N_HEADS, HEAD_DIM)
    k = qkv[..., N_HEADS * HEAD_DIM:(N_HEADS + N_KV_HEADS) * HEAD_DIM].reshape(B, T, N_KV_HEADS, HEAD_DIM)
    v = qkv[..., (N_HEADS + N_KV_HEADS) * HEAD_DIM:].reshape(B, T, N_KV_HEADS, HEAD_DIM)
    return rmsnorm(q, g_q), rmsnorm(k, g_k), v


def grid_rope_tables(n_tokens):
    rows = n_tokens // GRID_W
    row = jnp.broadcast_to(jnp.arange(rows, dtype=jnp.float32)[:, None], (rows, GRID_W)).reshape(-1)
    col = jnp.broadcast_to(jnp.arange(GRID_W, dtype=jnp.float32)[None, :], (rows, GRID_W)).reshape(-1)
    inv_freq = ROPE_THETA ** (-jnp.arange(ROT_PAIRS, dtype=jnp.float32) / ROT_PAIRS)
    ang_r = (row[:, None] * inv_freq[None, :])[:, None, :]
    ang_c = (col[:, None] * inv_freq[None, :])[:, None, :]
    return jnp.cos(ang_r), jnp.sin(ang_r), jnp.cos(ang_c), jnp.sin(ang_c)


def apply_rope2d(x, tables):
    cos_r, sin_r, cos_c, sin_c = tables
    xf = x.astype(jnp.float32)
    xr, xc = jnp.split(xf, 2, axis=-1)

    def rot(z, cos, sin):
        z1, z2 = jnp.split(z, 2, axis=-1)
        return jnp.concatenate([z1 * cos - z2 * sin, z2 * cos + z1 * sin], axis=-1)

    return jnp.concatenate([rot(xr, cos_r, sin_r), rot(xc, cos_c, sin_c)], axis=-1).astype(x.dtype)


def blocked_attention(q, k, v):
    B, S, H, hd = q.shape
    G = H // N_KV_HEADS
    nb = S // Q_BLOCK
    qb = q.reshape(B, nb, Q_BLOCK, N_KV_HEADS, G, hd).transpose(1, 0, 2, 3, 4, 5)
    scale = 1.0 / math.sqrt(hd)

    def one_block(qblk):
        s = jnp.einsum('bqkgd,btkd->bkgqt', qblk, k).astype(jnp.float32) * scale
        p = jax.nn.softmax(s, axis=-1).astype(v.dtype)
        return jnp.einsum('bkgqt,btkd->bqkgd', p, v)

    o = lax.map(one_block, qb)
    return o.transpose(1, 0, 2, 3, 4, 5).reshape(B, S, H * hd)


def squared_relu_mlp(x, w1, w2):
    hdn = jax.nn.relu(x @ w1)
    return (hdn * hdn) @ w2


def setup_inputs(seed: int = 0) -> dict:
    key = jax.random.key(seed)
    ks = iter(jax.random.split(key, 40))

    def nrm(shape, scale):
        return jax.random.normal(next(ks), shape, jnp.float32) * scale

    def gain(shape):
        return 1.0 + nrm(shape, 0.02)

    D = D_MODEL
    qkv_w = (N_HEADS + 2 * N_KV_HEADS) * HEAD_DIM
    a0 = jax.random.uniform(next(ks), (N_RG_LAYERS, 2, RNN_WIDTH), jnp.float32, 0.9, 0.999)
    p = a0 ** (1.0 / RG_C)
    rg_lambda = jnp.log(p) - jnp.log1p(-p)
    return {
        "x_prompt": nrm((BATCH, SEQ, D), 1.0),
        "x_sample": nrm((DEC_BATCH, DEC_SEQ, D), 1.0),
        "state_rglru": nrm((DEC_BATCH, N_RG_LAYERS, 2, RNN_WIDTH), 0.5),
        "cache_k": nrm((DEC_BATCH, N_ATTN_LAYERS, PAST_LEN, N_KV_HEADS, HEAD_DIM), 1.0),
        "cache_v": nrm((DEC_BATCH, N_ATTN_LAYERS, PAST_LEN, N_KV_HEADS, HEAD_DIM), 1.0),
        "c": nrm((DEC_BATCH, D), 1.0),
        "c_ctx": nrm((D,), 1.0),
        "w_mod": nrm((DEPTH, D, 6 * D), 0.5 * D ** -0.5),
        "b_mod": nrm((DEPTH, 6 * D), 0.01),
        "g_pre_mix": gain((DEPTH, D)),
        "g_post_mix": gain((DEPTH, D)),
        "g_pre_ffn": gain((DEPTH, D)),
        "g_post_ffn": gain((DEPTH, D)),
        "w_qkv": nrm((N_ATTN_LAYERS, D, qkv_w), D ** -0.5),
        "g_q": gain((N_ATTN_LAYERS, HEAD_DIM)),
        "g_k": gain((N_ATTN_LAYERS, HEAD_DIM)),
        "w_o": nrm((N_ATTN_LAYERS, N_HEADS * HEAD_DIM, D), (N_HEADS * HEAD_DIM) ** -0.5),
        "w_rg_in": nrm((N_RG_LAYERS, D, 2 * RNN_WIDTH), D ** -0.5),
        "rg_conv_w": nrm((N_RG_LAYERS, CONV_W, RNN_WIDTH), CONV_W ** -0.5),
        "rg_conv_b": nrm((N_RG_LAYERS, RNN_WIDTH), 0.01),
        "w_rg_a": nrm((N_RG_LAYERS, 2, RG_BLOCKS, RG_BLOCK_SIZE, RG_BLOCK_SIZE), RG_BLOCK_SIZE ** -0.5),
        "b_rg_a": nrm((N_RG_LAYERS, 2, RNN_WIDTH), 0.01),
        "w_rg_x": nrm((N_RG_LAYERS, 2, RG_BLOCKS, RG_BLOCK_SIZE, RG_BLOCK_SIZE), RG_BLOCK_SIZE ** -0.5),
        "b_rg_x": nrm((N_RG_LAYERS, 2, RNN_WIDTH), 0.01),
        "rg_lambda": rg_lambda,
        "w_rg_out": nrm((N_RG_LAYERS, RNN_WIDTH, D), RNN_WIDTH ** -0.5),
        "w_ff1": nrm((DEPTH, D, D_FF), D ** -0.5),
        "w_ff2": nrm((DEPTH, D_FF, D), D_FF ** -0.5),
    }


def reference(x_prompt, x_sample, state_rglru, cache_k, cache_v, c, c_ctx,
              w_mod, b_mod, g_pre_mix, g_post_mix, g_pre_ffn, g_post_ffn,
              w_qkv, g_q, g_k, w_o,
              w_rg_in, rg_conv_w, rg_conv_b, w_rg_a, b_rg_a, w_rg_x, b_rg_x, rg_lambda, w_rg_out,
              w_ff1, w_ff2):
    xp = x_prompt
    xs = x_sample
    rope_tab = grid_rope_tables(xs.shape[1])
    new_states, new_ks, new_vs = [], [], []

    for l in range(DEPTH):
        mp = modulation(c_ctx[None, :], w_mod[l], b_mod[l])
        ms = modulation(c, w_mod[l], b_mod[l])
        hp = rmsnorm(xp, g_pre_mix[l]) * (1.0 + mp[1]) + mp[0]
        hs = rmsnorm(xs, g_pre_mix[l]) * (1.0 + ms[1]) + ms[0]
        j = l // N_MIXERS
        if l % N_MIXERS == 0:
            rg = (w_rg_in[j], rg_conv_w[j], rg_conv_b[j], w_rg_a[j], b_rg_a[j],
                  w_rg_x[j], b_rg_x[j], rg_lambda[j], w_rg_out[j])
            h0 = jnp.zeros((xp.shape[0], 2, RNN_WIDTH), xp.dtype)
            op, st = rglru_mixer(hp, h0, *rg)
            os_, _ = rglru_mixer(hs, state_rglru[:, j], *rg)
            new_states.append(st)
        else:
            qp, kp, vp = qkv_heads(hp, w_qkv[j], g_q[j], g_k[j])
            op = blocked_attention(qp, kp, vp) @ w_o[j]
            new_ks.append(kp)
            new_vs.append(vp)
            qs, ks_, vs_ = qkv_heads(hs, w_qkv[j], g_q[j], g_k[j])
            qs = apply_rope2d(qs, rope_tab)
            ks_ = apply_rope2d(ks_, rope_tab)
            k_all = jnp.concatenate([ks_, cache_k[:, j]], axis=1)
            v_all = jnp.concatenate([vs_, cache_v[:, j]], axis=1)
            os_ = blocked_attention(qs, k_all, v_all) @ w_o[j]
        xp = xp + mp[2] * rmsnorm(op, g_post_mix[l])
        xs = xs + ms[2] * rmsnorm(os_, g_post_mix[l])
        fp = rmsnorm(xp, g_pre_ffn[l]) * (1.0 + mp[4]) + mp[3]
        fs = rmsnorm(xs, g_pre_ffn[l]) * (1.0 + ms[4]) + ms[3]
        xp = xp + mp[5] * rmsnorm(squared_relu_mlp(fp, w_ff1[l], w_ff2[l]), g_post_ffn[l])
        xs = xs + ms[5] * rmsnorm(squared_relu_mlp(fs, w_ff1[l], w_ff2[l]), g_post_ffn[l])

    new_state_rglru = jnp.stack(new_states, axis=1)
    new_cache_k = jnp.stack(new_ks, axis=1)
    new_cache_v = jnp.stack(new_vs, axis=1)
    return (xp, xs, new_state_rglru, new_cache_k, new_cache_v)
```

```python
import numpy as np
import concourse.bass as bass
import concourse.mybir as mybir
from concourse.bass_utils import run_bass_kernel_spmd

F32 = mybir.dt.float32
F32R = mybir.dt.float32r
BF16 = mybir.dt.bfloat16
MMT = BF16
AF = mybir.ActivationFunctionType
ALU = mybir.AluOpType

D = 2048
NCH = 16
L = 4
TB = 512
NBLK = 5
T = 2560
TS = 2048
PAST = 512
TK = T + PAST
DFF = 8192
EPS = 1e-6
QKVW = 3072
SEGS = [(0, 2048), (2048, 2304), (2304, 2560)]

OFF_BMOD = 0
OFF_GPM = 384
OFF_GQM = 448
OFF_GPF = 512
OFF_GQF = 576
OFF_CW = 640
OFF_CB = 768
OFF_BA = 800
OFF_BX = 864
OFF_LAM = 928
OFF_ST = 992
OFF_CVEC = 1056
OFF_GQ = 1088
OFF_GK = 1090
NSMALL = 1152

ENGS = ("pe", "act", "dve", "pool", "sp")
NDCH = 8


class Op:
    __slots__ = ("eng", "fn", "deps", "sig", "val", "dma", "sem", "prev")


class Sched:
    def __init__(self):
        self.ops = {e: [] for e in ENGS}
        self.keys = {}
        self.last = {e: None for e in ENGS}
        self.fence = {e: [] for e in ENGS}
        self.recent_dma = {e: [] for e in ENGS}

    def add(self, eng, fn, reads=(), writes=(), dma=False):
        op = Op()
        op.eng = eng
        op.fn = fn
        op.dma = dma
        op.sig = dma
        op.val = 0
        op.sem = None
        op.prev = None
        deps = []
        for k in reads:
            st = self.keys.get(k)
            if st is not None and st[0] is not None:
                deps.append(st[0])
        for k in writes:
            st = self.keys.get(k)
            if st is not None:
                if st[0] is not None:
                    deps.append(st[0])
                deps.extend(st[1])
        deps.extend(self.fence[eng])
        self.fence[eng] = []
        seen = set()
        od = []
        for d in deps:
            if id(d) in seen:
                continue
            seen.add(id(d))
            if d.eng == "pe" and eng == "pe" and not d.dma:
                continue
            od.append(d)
        op.deps = od
        for d in od:
            d.sig = True
        for k in reads:
            st = self.keys.get(k)
            if st is None:
                self.keys[k] = [None, [op]]
            else:
                st[1].append(op)
        for k in writes:
            self.keys[k] = [op, []]
        self.ops[eng].append(op)
        self.last[eng] = op
        if dma:
            self.recent_dma[eng] = (self.recent_dma[eng] + [op])[-NDCH:]
        return op

    def barrier(self):
        lasts = [o for o in self.last.values() if o is not None]
        for e in ENGS:
            lasts.extend(self.recent_dma[e])
        for e in ENGS:
            self.fence[e] = list(lasts)

    def emit(self, nc, block, sems, dsems):
        for e in ENGS:
            v = 0
            chc = [0] * NDCH
            chlast = [None] * NDCH
            n = 0
            for op in self.ops[e]:
                if op.dma:
                    ch = n % NDCH
                    n += 1
                    chc[ch] += 1
                    op.sem = dsems[e][ch]
                    op.val = 16 * chc[ch]
                    op.prev = chlast[ch]
                    chlast[ch] = op
                elif op.sig:
                    v += 1
                    op.val = v
                    op.sem = sems[e]
        finals = []
        for e in ENGS:
            for op in self.ops[e]:
                if op.dma:
                    finals.append(op)
        fin = {}
        for op in finals:
            fin[id(op.sem)] = (op.sem, max(op.val, fin.get(id(op.sem), (None, 0))[1]))

        def run(h, e):
            seen = {}
            for op in self.ops[e]:
                waits = [(d.sem, d.val) for d in op.deps]
                if op.dma and op.prev is not None:
                    waits.append((op.prev.sem, op.prev.val))
                for sem, val in waits:
                    if seen.get(id(sem), 0) < val:
                        h.wait_ge(sem, val)
                        seen[id(sem)] = val
                ins = op.fn(h)
                if op.dma:
                    ins.then_inc(op.sem, 16)
                elif op.sig:
                    ins.then_inc(op.sem, 1)
            if e == "sp":
                for sem, val in fin.values():
                    if seen.get(id(sem), 0) < val:
                        h.wait_ge(sem, val)

        @block.tensor
        def _(h):
            run(h, "pe")

        @block.scalar
        def _(h):
            run(h, "act")

        @block.vector
        def _(h):
            run(h, "dve")

        @block.gpsimd
        def _(h):
            run(h, "pool")

        @block.sync
        def _(h):
            run(h, "sp")


class Ring:
    def __init__(self, items):
        self.items = items
        self.i = 0

    def next(self):
        it = self.items[self.i % len(self.items)]
        self.i += 1
        return it


def build(depth=L):
    nc = bass.Bass("TRN2", target_bir_lowering=False)
    S = Sched()

    def din(name, shape):
        return nc.dram_tensor(name, list(shape), F32, kind="ExternalInput").ap()

    def dout(name, shape):
        return nc.dram_tensor(name, list(shape), F32, kind="ExternalOutput").ap()

    def dscr(name, shape):
        return nc.dram_tensor(name, list(shape), F32).ap()

    x_tok = din("x_tok", [T, D])
    small = din("small", [NSMALL, 128])
    ck = din("ck", [2, PAST, 512])
    cv = din("cv", [2, PAST, 512])
    consts = din("consts", [128, 3 * 128 + 2 * TS])
    w_mod = din("w_mod", [L, D, 6 * D])
    w_qkv = din("w_qkv", [2, D, QKVW])
    w_o = din("w_o", [2, D, D])
    w_rg_in = din("w_rg_in", [2, D, 2 * D])
    w_rg_a = din("w_rg_a", [2, 2, 16, 128, 128])
    w_rg_x = din("w_rg_x", [2, 2, 16, 128, 128])
    w_rg_out = din("w_rg_out", [2, D, D])
    w_ff1 = din("w_ff1", [L, D, DFF])
    w_ff2 = din("w_ff2", [L, DFF, D])

    y_tok = dout("y_tok", [T, D])
    nstate = dout("nstate", [128, 128])
    nk = dout("nk", [2, 2, 256, 512])
    nv = dout("nv", [2, 2, 256, 512])

    xres = dscr("xres", [NBLK, 128, NCH * TB])
    gate_s = dscr("gate_s", [NCH, 128, T])
    xb_s = dscr("xb_s", [NCH, 128, T])
    y_s = dscr("y_s", [NCH, 128, T])
    q_s = dscr("q_s", [NBLK, 128, NCH * TB])
    kT_s = dscr("kT_s", [4, 128, TK])
    v_s = dscr("v_s", [TK, 512])

    from contextlib import ExitStack
    es = ExitStack()

    def sb(name, shape, dt=F32):
        return es.enter_context(nc.sbuf_tensor(name, list(shape), dt))

    wring = sb("wring", [128, 6 * 4096], MMT)
    XT = sb("XT", [128, 8192], F32)
    HT = sb("HT", [128, 8192], MMT)
    XC = sb("XC", [128, T], F32)
    TB1 = sb("TB1", [128, T], F32)
    TB2 = sb("TB2", [128, T], F32)
    OT = sb("OT", [128, 8192], F32)
    smallT = sb("smallT", [128, NSMALL])
    modT = sb("modT", [128, L * 2 * 96])
    dm = sb("dm", [128, L * 2 * 4 * 16])
    spT = sb("spT", [128, 128])
    sl = sb("sl", [128, 16, 2], MMT)
    sl32 = sb("sl32", [128, 16, 2], F32)
    ident = sb("ident", [128, 128])
    ones = sb("ones", [128, 128], MMT)
    prot = sb("prot", [128, 128], MMT)
    cs = sb("cs", [128, 2, TB])
    stout = sb("stout", [128, 128])
    cst = sb("cst", [128, 4])
    sq_t = [sb("sq%d" % i, [128, TB], MMT) for i in range(4)]
    tmp_t = [sb("tmp%d" % i, [128, TB]) for i in range(3)]
    rstd_t = [sb("rstd%d" % i, [128, TB]) for i in range(2)]
    hid_t = [sb("hid%d" % i, [128, 2, TB], MMT) for i in range(2)]
    pT_t = [sb("pT%d" % i, [128, TB], MMT) for i in range(3)]
    gw_t = [sb("gw%d" % i, [128, 4, 128], MMT) for i in range(2)]
    ps_t = [es.enter_context(nc.psum_tensor("ps%d" % i, [128, TB], F32)) for i in range(8)]
    sems = {e: es.enter_context(nc.semaphore("s_" + e)) for e in ENGS}
    dsems = {e: [es.enter_context(nc.semaphore("d_%s%d" % (e, i))) for i in range(NDCH)] for e in ENGS}
    block = es.enter_context(nc.Block())

    tmp_t = tmp_t + [TB1[:, i * TB:(i + 1) * TB] for i in range(5)]
    rstd_t = rstd_t + [TB2[:, i * TB:(i + 1) * TB] for i in range(4)]
    sqR = Ring(list(range(4)))
    tmpR = Ring(list(range(8)))
    rstdR = Ring(list(range(6)))
    hidR = Ring(list(range(2)))
    pTR = Ring(list(range(3)))
    gwR = Ring(list(range(2)))
    psR = Ring(list(range(8)))
    psLo = Ring([0, 1, 2, 3])
    accR = Ring([(4, 5), (6, 7)])
    wR = Ring(list(range(6)))

    Xv = XT[:].rearrange("p (c t) -> p c t", c=NCH)
    Ov = OT[:].rearrange("p (c t) -> p c t", c=NCH)
    Hr = HT[:].rearrange("p (c t) -> p c t", c=NCH)
    REG = {"X": Xv, "O": Ov}
    REGR = {"H": Hr}

    def rk(reg, c=None):
        if c is None:
            return [(reg, i) for i in range(NCH)]
        return [(reg, c)]

    def wslot(i):
        return wring[:, i * 4096:(i + 1) * 4096]

    A = S.add

    def newps(lo=False):
        i = psLo.next() if lo else psR.next()
        return ps_t[i], ("ps", i)

    def dma(eng, out, in_, reads, writes):
        if eng == "pool":
            A(eng, lambda h, o=out, i=in_: h.dma_start(out=o, in_=i, max_dma_last_dim=8192), reads=reads,
              writes=writes, dma=True)
        else:
            A(eng, lambda h, o=out, i=in_: h.dma_start(out=o, in_=i), reads=reads, writes=writes, dma=True)

    def act(out, in_, func, reads, writes, bias=None, scale=1.0):
        def f(h, o=out, i=in_, fn=func, b=bias, s=scale):
            if b is None:
                return h.activation(out=o, in_=i, func=fn, scale=s)
            return h.activation(out=o, in_=i, func=fn, bias=b, scale=s)
        A("act", f, reads=reads, writes=writes)

    def tt(out, in0, in1, op, reads, writes):
        A("dve", lambda h, o=out, a=in0, b=in1, p=op: h.tensor_tensor(o, a, b, p), reads=reads, writes=writes)

    def stt(out, in0, scalar, in1, op0, op1, reads, writes):
        A("dve", lambda h, o=out, a=in0, s=scalar, b=in1, p0=op0, p1=op1:
          h.scalar_tensor_tensor(o, a, s, b, p0, p1), reads=reads, writes=writes)

    def ts(out, in0, s1, s2, op0, op1, reads, writes):
        if s2 is None:
            A("dve", lambda h, o=out, a=in0, x=s1, p0=op0: h.tensor_scalar(o, a, x, None, p0),
              reads=reads, writes=writes)
        else:
            A("dve", lambda h, o=out, a=in0, x=s1, y=s2, p0=op0, p1=op1: h.tensor_scalar(o, a, x, y, p0, p1),
              reads=reads, writes=writes)

    def mm(ps, lhsT, rhs, start, stop, reads, writes):
        A("pe", lambda h, o=ps, l=lhsT, r=rhs, s0=start, s1=stop: h.matmul(o, l, r, start=s0, stop=s1),
          reads=reads, writes=writes)

    def mmgroup(ps, pairs, reads, writes):
        def f(h, o=ps, pr=pairs):
            n = len(pr)
            ins = None
            for i, (l, r) in enumerate(pr):
                ins = h.matmul(o, l, r, start=(i == 0), stop=(i == n - 1))
            return ins
        A("pe", f, reads=reads, writes=writes)

    def transp(ps, in_, idn, reads, writes):
        A("pe", lambda h, o=ps, i=in_, d=idn: h.transpose(o, i, d), reads=reads, writes=writes)

    onec = cst[:, 0:1]
    epsc = cst[:, 1:2]

    def rstd_from(srcs, inv_n):
        W = srcs[0][0].shape[-1]
        ps, pk = newps()
        n = len(srcs)
        for c, (ap, keys) in enumerate(srcs):
            si = sqR.next()
            act(sq_t[si][:, 0:W], ap, AF.Square, reads=keys, writes=[("sq", si)])
            mm(ps[:, 0:W], ones[:], sq_t[si][:, 0:W], c == 0, c == n - 1, reads=[("sq", si), "ones"], writes=[pk])
        ti = tmpR.next()
        act(tmp_t[ti][:, 0:W], ps[:, 0:W], AF.Sqrt, reads=[pk, "cst"], writes=[("tmp", ti)], bias=epsc, scale=inv_n)
        ri = rstdR.next()
        A("dve", lambda h, o=rstd_t[ri][:, 0:W], i=tmp_t[ti][:, 0:W]: h.reciprocal(o, i),
          reads=[("tmp", ti)], writes=[("rstd", ri)])
        return ri

    def dmcol(l, v, which, c):
        base = ((l * 2 + v) * 4 + which) * 16 + c
        return dm[:, base:base + 1]

    def modcol(l, v, m, c):
        base = (l * 2 + v) * 96 + m * 16 + c
        return modT[:, base:base + 1]

    def norm_mod(l, v, sub):
        ri = rstd_from([(Xv[:, c, :], rk("X", c)) for c in range(NCH)], 1.0 / D)
        for c in range(NCH):
            ti = tmpR.next()
            stt(tmp_t[ti][:], Xv[:, c, :], dmcol(l, v, 2 * sub, c), rstd_t[ri][:], ALU.mult, ALU.mult,
                reads=rk("X", c) + [("rstd", ri), ("dm", l)], writes=[("tmp", ti)])
            act(Hr[:, c, :], tmp_t[ti][:], AF.Identity, reads=[("tmp", ti), ("modT", l)], writes=rk("H", c),
                bias=modcol(l, v, 3 * sub, c))

    def post_norm_res(l, v, sub, oreg):
        ov = REG[oreg]
        ri = rstd_from([(ov[:, c, :], rk(oreg, c)) for c in range(NCH)], 1.0 / D)
        for c in range(NCH):
            ti = tmpR.next()
            stt(tmp_t[ti][:], ov[:, c, :], dmcol(l, v, 2 * sub + 1, c), rstd_t[ri][:], ALU.mult, ALU.mult,
                reads=rk(oreg, c) + [("rstd", ri), ("dm", l)], writes=[("tmp", ti)])
            tt(Xv[:, c, :], Xv[:, c, :], tmp_t[ti][:], ALU.add, reads=rk("X", c) + [("tmp", ti)], writes=rk("X", c))

    def load_w(dram_ap, shape3):
        wi = wR.next()
        a, b = shape3
        view = wslot(wi)[:, 0:a * b].rearrange("p (a b) -> p a b", a=a)
        if b > 512:
            dma("pool", wslot(wi)[:, 0:a * b].rearrange("p (a c d) -> p a c d", a=a, d=512),
                dram_ap.rearrange("p a (c d) -> p a c d", d=512), reads=[], writes=[("w", wi)])
        else:
            dma("pool", view, dram_ap, reads=[], writes=[("w", wi)])
        return view, ("w", wi)

    def linear(inreg, wmat, col0, nchunks, evac):
        hin = REGR[inreg]
        wv = wmat.rearrange("(kc p) n -> p kc n", p=128)
        for t in range(nchunks // 2):
            view, wk = load_w(wv[:, :, col0 + t * 256: col0 + (t + 1) * 256], (16, 256))
            for j in range(2):
                ps, pk = newps()
                mmgroup(ps[:], [(view[:, k, j * 128:(j + 1) * 128], hin[:, k, :]) for k in range(NCH)],
                        reads=[wk] + rk(inreg), writes=[pk])
                evac(2 * t + j, ps, pk)

    def ffn(l, v):
        norm_mod(l, v, 1)
        w1v = w_ff1[l].rearrange("(kc p) n -> p kc n", p=128)
        w2v = w_ff2[l].rearrange("(g kc p) n -> g p kc n", kc=2, p=128)
        def ffn2(g, hi):
            view2, wk2 = load_w(w2v[g], (2, 2048))
            for n in range(NCH):
                ps, pk = newps()
                mmgroup(ps[:], [(view2[:, kc, n * 128:(n + 1) * 128], hid_t[hi][:, kc, :]) for kc in range(2)],
                        reads=[wk2, ("hid", hi, 0), ("hid", hi, 1)], writes=[pk])
                if g == 0:
                    act(Ov[:, n, :], ps[:], AF.Copy, reads=[pk], writes=rk("O", n))
                else:
                    tt(Ov[:, n, :], Ov[:, n, :], ps[:], ALU.add, reads=[pk] + rk("O", n), writes=rk("O", n))
        prev = None
        for g in range(DFF // 256):
            view, wk = load_w(w1v[:, :, g * 256:(g + 1) * 256], (16, 256))
            hi = hidR.next()
            for j in range(2):
                ps, pk = newps()
                mmgroup(ps[:], [(view[:, k, j * 128:(j + 1) * 128], Hr[:, k, :]) for k in range(NCH)],
                        reads=[wk] + rk("H"), writes=[pk])
                ti = tmpR.next()
                act(tmp_t[ti][:], ps[:], AF.Relu, reads=[pk], writes=[("tmp", ti)])
                tt(hid_t[hi][:, j, :], tmp_t[ti][:], tmp_t[ti][:], ALU.mult, reads=[("tmp", ti)],
                   writes=[("hid", hi, j)])
            if prev is not None:
                ffn2(*prev)
            prev = (g, hi)
        ffn2(*prev)
        post_norm_res(l, v, 1, "O")

    def load_x(blk):
        dma("sp", XT[:], xres[blk], reads=[("xres", blk)], writes=rk("X"))

    def store_x(blk):
        dma("sp", xres[blk], XT[:], reads=rk("X"), writes=[("xres", blk)])

    A("dve", lambda h: h.memset(cst[:, 0:1], 1.0), writes=["cst0"])
    A("dve", lambda h: h.memset(cst[:, 1:2], EPS), reads=["cst0"], writes=["cst"])
    dma("sp", ident[:], consts[:, 0:128], reads=[], writes=["ident"])
    dma("pool", ones[:], consts[:, 128:256], reads=[], writes=["ones"])
    dma("pool", prot[:], consts[:, 256:384], reads=[], writes=["prot"])
    for i in range(NSMALL // 128):
        ti = tmpR.next()
        dma("sp", tmp_t[ti][:, 0:128], small[i * 128:(i + 1) * 128, :], reads=[], writes=[("tmp", ti)])
        ps, pk = newps()
        transp(ps[:, 0:128], tmp_t[ti][:, 0:128], ident[:], reads=[("tmp", ti), "ident"], writes=[pk])
        act(smallT[:, i * 128:(i + 1) * 128], ps[:, 0:128], AF.Copy, reads=[pk], writes=["smallT"])
    for v in range(2):
        act(sl[:, :, v], smallT[:, OFF_CVEC + v * 16: OFF_CVEC + (v + 1) * 16], AF.Silu, reads=["smallT"],
            writes=[("sl", v)])
        act(sl32[:, :, v], smallT[:, OFF_CVEC + v * 16: OFF_CVEC + (v + 1) * 16], AF.Silu, reads=["smallT"],
            writes=[("sl32", v)])
    act(spT[:, 0:64], smallT[:, OFF_LAM:OFF_LAM + 64], AF.Exp, reads=["smallT"], writes=["spT"], scale=-1.0)
    act(spT[:, 0:64], spT[:, 0:64], AF.Ln, reads=["spT", "cst"], writes=["spT"], bias=onec)
    ts(spT[:, 64:128], spT[:, 0:64], -16.0, None, ALU.mult, None, reads=["spT"], writes=["spT2"])
    ts(spT[:, 0:64], spT[:, 0:64], -8.0, None, ALU.mult, None, reads=["spT", "spT2"], writes=["spT"])
    def mod_unit(l, t):
        wv = w_mod[l].rearrange("(kc p) n -> p kc n", p=128)
        view, wk = load_w(wv[:, :, t * 256:(t + 1) * 256], (16, 256))
        for j in range(2):
            cc = 2 * t + j
            ps, pk = newps()
            mmgroup(ps[:, 0:2], [(view[:, k, j * 128:(j + 1) * 128], sl[:, k, :]) for k in range(NCH)],
                    reads=[wk, ("sl", 0), ("sl", 1)], writes=[pk])
            mview = modT[:, l * 192: (l + 1) * 192].rearrange("p (v c) -> p v c", v=2)[:, :, cc]
            bcol = OFF_BMOD + l * 96 + cc
            ts(mview, ps[:, 0:2], smallT[:, bcol:bcol + 1], None, ALU.add, None,
               reads=[pk, "smallT"], writes=[("modT", l)])

    def mod_derive(l):
        for v in range(2):
            for sub in range(2):
                gpre = OFF_GPM if sub == 0 else OFF_GPF
                gpost = OFF_GQM if sub == 0 else OFF_GQF
                mbase = (l * 2 + v) * 96 + sub * 48
                dbase = ((l * 2 + v) * 4 + 2 * sub) * 16
                stt(dm[:, dbase:dbase + 16], modT[:, mbase + 16: mbase + 32], 1.0,
                    smallT[:, gpre + l * 16: gpre + (l + 1) * 16], ALU.add, ALU.mult,
                    reads=[("modT", l), "smallT"], writes=[("dm", l)])
                tt(dm[:, dbase + 16:dbase + 32], modT[:, mbase + 32: mbase + 48],
                   smallT[:, gpost + l * 16: gpost + (l + 1) * 16], ALU.mult,
                   reads=[("modT", l), "smallT", ("dm", l)], writes=[("dm", l)])

    stR = Ring([0, 1])
    bfR = Ring([4, 5])

    def mod_unit_hw(l, t):
        p = stR.next()
        q = bfR.next()
        wv = w_mod[l].rearrange("(kc p) n -> p kc n", p=128)
        stage = wring[:, p * 8192:(p + 1) * 8192].bitcast(F32)
        skeys = [("w", 2 * p), ("w", 2 * p + 1)]
        dma("sp", stage.rearrange("p (a b) -> p a b", a=16), wv[:, :, t * 256:(t + 1) * 256], reads=[],
            writes=skeys)
        wb = wslot(q)
        if (q % 2) == 0:
            act(wb, stage, AF.Copy, reads=skeys, writes=[("w", q)])
        else:
            A("dve", lambda h, o=wb, i=stage: h.tensor_copy(o, i), reads=skeys, writes=[("w", q)])
        view = wb.rearrange("p (a b) -> p a b", a=16)
        for j in range(2):
            cc = 2 * t + j
            ps, pk = newps()
            mmgroup(ps[:, 0:2], [(view[:, k, j * 128:(j + 1) * 128], sl[:, k, :]) for k in range(NCH)],
                    reads=[("w", q), ("sl", 0), ("sl", 1)], writes=[pk])
            mview = modT[:, l * 192: (l + 1) * 192].rearrange("p (v c) -> p v c", v=2)[:, :, cc]
            bcol = OFF_BMOD + l * 96 + cc
            ts(mview, ps[:, 0:2], smallT[:, bcol:bcol + 1], None, ALU.add, None,
               reads=[pk, "smallT"], writes=[("modT", l)])

    mod_pending = []
    for l in range(depth):
        if l == 0:
            for t in range(48):
                mod_unit(l, t)
            mod_derive(l)
        else:
            mod_pending.extend([(l, t) for t in range(48)] + [(l, None)])

    def mod_drain(nunits):
        for _ in range(nunits):
            if not mod_pending:
                return
            l, t = mod_pending.pop(0)
            if t is None:
                mod_derive(l)
            else:
                mod_unit_hw(l, t)

    for blk in range(NBLK):
        for tq in range(4):
            r0 = blk * TB + tq * 128
            dma("sp", OT[:, tq * 2048: (tq + 1) * 2048], x_tok[r0:r0 + 128, :], reads=[],
                writes=[("O", 4 * tq + i) for i in range(4)])
        for c in range(NCH):
            ps, pk = newps()
            for tq in range(4):
                src = OT[:, tq * 2048 + c * 128: tq * 2048 + (c + 1) * 128]
                transp(ps[:, tq * 128:(tq + 1) * 128], src, ident[:], reads=rk("O") + ["ident"], writes=[pk])
            act(Xv[:, c, :], ps[:], AF.Copy, reads=[pk], writes=rk("X", c))
        store_x(blk)

    def rg_layer(l):
        jl = l // 2
        for blk in range(NBLK):
            v = 0 if blk < 4 else 1
            load_x(blk)
            norm_mod(l, v, 0)

            def evac(m, ps, pk, blk=blk):
                ti = tmpR.next()
                act(tmp_t[ti][:], ps[:], AF.Copy, reads=[pk], writes=[("tmp", ti)])
                dst = gate_s if m < 16 else xb_s
                n = m % 16
                dma("sp", dst[n][:, blk * TB:(blk + 1) * TB], tmp_t[ti][:], reads=[("tmp", ti)],
                    writes=[("gx", m, blk)])
            linear("H", w_rg_in[jl], 0, 32, evac)
        S.barrier()
        Tt = [XT[:, i * T:(i + 1) * T] for i in range(3)] + [OT[:, i * T:(i + 1) * T] for i in range(3)]
        Tt.append(XC[:])
        T7r = HT[:, 0:T]
        for n in range(NCH):
            xb, xc = Tt[5], Tt[6]
            dma("pool", xb, xb_s[n], reads=[("gx", 16 + n, b) for b in range(NBLK)], writes=["T5"])
            gi = gwR.next()
            dma("pool", gw_t[gi][:, 0:2, :], w_rg_a[jl, :, n].rearrange("e j k -> j e k"), reads=[],
                writes=[("gwa", gi)])
            dma("pool", gw_t[gi][:, 2:4, :], w_rg_x[jl, :, n].rearrange("e j k -> j e k"), reads=[],
                writes=[("gwx", gi)])
            cwc = lambda k: smallT[:, OFF_CW + jl * 64 + k * 16 + n: OFF_CW + jl * 64 + k * 16 + n + 1]
            cbc = smallT[:, OFF_CB + jl * 16 + n: OFF_CB + jl * 16 + n + 1]
            for (s0, s1) in SEGS:
                ts(xc[:, s0:s1], xb[:, s0:s1], cwc(1), cbc, ALU.mult, ALU.add, reads=["T5", "smallT"],
                   writes=["T6"])
                for k in (0, 2, 3):
                    o = k - 1
                    a0 = max(s0, s0 - o)
                    a1 = min(s1, s1 - o)
                    stt(xc[:, a0:a1], xb[:, a0 + o:a1 + o], cwc(k), xc[:, a0:a1], ALU.mult, ALU.add,
                        reads=["T5", "T6", "smallT"], writes=["T6"])
            act(T7r, xc, AF.Copy, reads=["T6"], writes=["T6r"])
            deferred = []
            for e in range(2):
                t1, t2 = (Tt[0], Tt[1]) if e == 0 else (TB1[:], TB2[:])
                k1, k2 = ("T0", "T1") if e == 0 else ("TB1", "TB2")
                t3 = Tt[2] if e == 0 else Tt[3]
                k3 = "T2" if e == 0 else "T3"
                bacol = OFF_BA + (jl * 2 + e) * 16 + n
                bxcol = OFF_BX + (jl * 2 + e) * 16 + n
                spcol = (jl * 2 + e) * 16 + n
                for blk in range(NBLK):
                    c0, c1 = blk * TB, (blk + 1) * TB
                    ps, pk = newps()
                    mm(ps[:], gw_t[gi][:, e, :], T7r[:, c0:c1], True, True, reads=[("gwa", gi), "T6r"], writes=[pk])
                    act(t1[:, c0:c1], ps[:], AF.Sigmoid, reads=[pk, "smallT"], writes=[k1],
                        bias=smallT[:, bacol:bacol + 1])
                    ps, pk = newps()
                    mm(ps[:], gw_t[gi][:, 2 + e, :], T7r[:, c0:c1], True, True, reads=[("gwx", gi), "T6r"],
                       writes=[pk])
                    act(t2[:, c0:c1], ps[:], AF.Sigmoid, reads=[pk, "smallT"], writes=[k2],
                        bias=smallT[:, bxcol:bxcol + 1])
                    mod_drain(1)
                act(t3, t1, AF.Exp, reads=[k1, "spT2"], writes=[k3], scale=spT[:, 64 + spcol: 64 + spcol + 1])
                act(t3, t3, AF.Sqrt, reads=[k3, "cst"], writes=[k3], bias=onec, scale=-1.0)
                act(t1, t1, AF.Exp, reads=[k1, "spT"], writes=[k1], scale=spT[:, spcol:spcol + 1])
                tt(t2, t2, t3, ALU.mult, reads=[k2, k3], writes=[k2])
                tt(t2, t2, xc, ALU.mult, reads=[k2, "T6"], writes=[k2])
                for si, (s0, s1) in enumerate(SEGS):
                    if si == 0:
                        init = smallT[:, OFF_ST + (jl * 2 + e) * 16 + n: OFF_ST + (jl * 2 + e) * 16 + n + 1]
                    else:
                        init = 0.0
                    if e == 0:
                        o_, a_, u_ = t3[:, s0:s1], t1[:, s0:s1], t2[:, s0:s1]
                    else:
                        o_, a_, u_ = t3[:, s0:s1][:, ::-1], t1[:, s0:s1][:, ::-1], t2[:, s0:s1][:, ::-1]
                    A("dve", lambda h, o=o_, a=a_, u=u_, i0=init: h.tensor_tensor_scan(o, a, u, i0, ALU.mult, ALU.add),
                      reads=[k1, k2, "smallT"], writes=[k3])
                    if si > 0:
                        col = (((si - 1) * 2 + jl) * 2 + e) * 16 + n
                        pos = s1 - 1 if e == 0 else s0
                        deferred.append((stout[:, col:col + 1], t3[:, pos:pos + 1], k3))
            dma("pool", Tt[4], gate_s[n], reads=[("gx", n, b) for b in range(NBLK)], writes=["T4"])
            act(Tt[4], Tt[4], AF.Gelu_apprx_tanh, reads=["T4"], writes=["T4"])
            for (o_, i_, kk) in deferred:
                act(o_, i_, AF.Copy, reads=[kk], writes=["stout"])
            tt(Tt[2], Tt[2], Tt[3], ALU.add, reads=["T2", "T3"], writes=["T2"])
            tt(Tt[2], Tt[2], Tt[4], ALU.mult, reads=["T2", "T4"], writes=["T2"])
            dma("pool", y_s[n], Tt[2], reads=["T2"], writes=[("y", n)])
        S.barrier()
        for blk in range(NBLK):
            v = 0 if blk < 4 else 1
            dma("pool", Hr, y_s[:, :, blk * TB:(blk + 1) * TB].rearrange("c p t -> p c t"),
                reads=[("y", n) for n in range(NCH)], writes=rk("H"))
            load_x(blk)

            def evac(m, ps, pk):
                act(Ov[:, m, :], ps[:], AF.Copy, reads=[pk], writes=rk("O", m))
            linear("H", w_rg_out[jl], 0, 16, evac)
            post_norm_res(l, v, 0, "O")
            ffn(l, v)
            store_x(blk)
        S.barrier()

    def attn_layer(l):
        jl = l // 2
        dma("sp", v_s[T:TK, :], cv[jl], reads=[], writes=[("v", 5)])
        for tq in range(4):
            ti = tmpR.next()
            dma("sp", tmp_t[ti][:], ck[jl, tq * 128:(tq + 1) * 128, :], reads=[], writes=[("tmp", ti)])
            ps, pk = newps()
            for g in range(4):
                transp(ps[:, g * 128:(g + 1) * 128], tmp_t[ti][:, g * 128:(g + 1) * 128], ident[:],
                       reads=[("tmp", ti), "ident"], writes=[pk])
            t2i = tmpR.next()
            act(tmp_t[t2i][:], ps[:], AF.Copy, reads=[pk], writes=[("tmp", t2i)])
            dma("sp", kT_s[:, :, T + tq * 128: T + (tq + 1) * 128].rearrange("g p t -> p g t"),
                tmp_t[t2i][:].rearrange("p (g t) -> p g t", g=4), reads=[("tmp", t2i)], writes=[("kc", tq)])
        for blk in range(NBLK):
            v = 0 if blk < 4 else 1
            load_x(blk)
            norm_mod(l, v, 0)
            if blk < 4:
                dma("sp", cs[:, 0, :], consts[:, 384 + blk * TB: 384 + (blk + 1) * TB], reads=[], writes=["cs0"])
                dma("sp", cs[:, 1, :], consts[:, 384 + TS + blk * TB: 384 + TS + (blk + 1) * TB], reads=[],
                    writes=["cs1"])

            def evac_a(m, ps, pk, blk=blk):
                isq = m < 16
                gcol = (OFF_GQ if isq else OFF_GK) + jl
                ri = rstd_from([(ps[:], [pk])], 1.0 / 128)
                ti = tmpR.next()
                si = None
                stt(tmp_t[ti][:], ps[:], smallT[:, gcol:gcol + 1], rstd_t[ri][:], ALU.mult, ALU.mult,
                    reads=[pk, ("rstd", ri), "smallT"], writes=[("tmp", ti)])
                if blk < 4:
                    si = sqR.next()
                    act(sq_t[si][:], tmp_t[ti][:], AF.Copy, reads=[("tmp", ti)], writes=[("sq", si)])
                return (m, ti, si)

            def evac_b(st, blk=blk):
                m, ti, si = st
                isq = m < 16
                if blk < 4:
                    ps2, pk2 = newps()
                    mm(ps2[:], prot[:], sq_t[si][:], True, True, reads=[("sq", si), "prot"], writes=[pk2])
                    t2i = tmpR.next()
                    tt(tmp_t[t2i][:], ps2[:], cs[:, 1, :], ALU.mult, reads=[pk2, "cs1"], writes=[("tmp", t2i)])
                    tt(tmp_t[ti][:], tmp_t[ti][:], cs[:, 0, :], ALU.mult, reads=[("tmp", ti), "cs0"],
                       writes=[("tmp", ti)])
                    tt(tmp_t[ti][:], tmp_t[ti][:], tmp_t[t2i][:], ALU.add, reads=[("tmp", ti), ("tmp", t2i)],
                       writes=[("tmp", ti)])
                if isq:
                    dma("sp", q_s[blk][:, m * TB:(m + 1) * TB], tmp_t[ti][:], reads=[("tmp", ti)],
                        writes=[("q", blk, m)])
                else:
                    g = m - 16
                    dma("sp", kT_s[g][:, blk * TB:(blk + 1) * TB], tmp_t[ti][:], reads=[("tmp", ti)],
                        writes=[("k", blk, g)])
                    if blk == 4:
                        ps3, pk3 = newps()
                        for tq in range(4):
                            transp(ps3[:, tq * 128:(tq + 1) * 128], tmp_t[ti][:, tq * 128:(tq + 1) * 128], ident[:],
                                   reads=[("tmp", ti), "ident"], writes=[pk3])
                        t3i = tmpR.next()
                        act(tmp_t[t3i][:], ps3[:], AF.Copy, reads=[pk3], writes=[("tmp", t3i)])
                        for sg in range(2):
                            dma("sp", nk[sg, jl][:, g * 128:(g + 1) * 128].rearrange("(a p) d -> p a d", p=128),
                                tmp_t[t3i][:, sg * 256:(sg + 1) * 256].rearrange("p (a d) -> p a d", a=2),
                                reads=[("tmp", t3i)], writes=[("nk", jl, g, sg)])

            pend_a, pend_b = [], []
            wvq = w_qkv[jl].rearrange("(kc p) n -> p kc n", p=128)
            for t in range(10):
                view, wk = load_w(wvq[:, :, t * 256:(t + 1) * 256], (16, 256))
                for j in range(2):
                    ps, pk = newps()
                    mmgroup(ps[:], [(view[:, k, j * 128:(j + 1) * 128], Hr[:, k, :]) for k in range(NCH)],
                            reads=[wk] + rk("H"), writes=[pk])
                    pend_a.append((2 * t + j, ps, pk))
                    if len(pend_a) > 1:
                        pend_b.append(evac_a(*pend_a.pop(0)))
                    if len(pend_b) > 1:
                        evac_b(pend_b.pop(0))
            while pend_a:
                pend_b.append(evac_a(*pend_a.pop(0)))
            while pend_b:
                evac_b(pend_b.pop(0))
            wv = w_qkv[jl].rearrange("(kc p) n -> p kc n", p=128)
            vh = []
            for half in range(2):
                view, wk = load_w(wv[:, :, 2560 + half * 256: 2560 + (half + 1) * 256], (16, 256))
                vh.append((view, wk))
            for tq in range(4):
                ps, pk = newps()
                for half in range(2):
                    view, wk = vh[half]
                    mmgroup(ps[:, half * 256:(half + 1) * 256],
                            [(Hr[:, k, tq * 128:(tq + 1) * 128], view[:, k, :]) for k in range(NCH)],
                            reads=[wk] + rk("H"), writes=[pk])
                ti = tmpR.next()
                act(tmp_t[ti][:], ps[:], AF.Copy, reads=[pk], writes=[("tmp", ti)])
                r0 = blk * TB + tq * 128
                dma("sp", v_s[r0:r0 + 128, :], tmp_t[ti][:], reads=[("tmp", ti)], writes=[("v", blk)])
                if blk == 4:
                    dma("sp", nv[tq // 2, jl, (tq % 2) * 128:(tq % 2 + 1) * 128, :], tmp_t[ti][:],
                        reads=[("tmp", ti)], writes=[("nv", jl, tq)])
        S.barrier()
        scale = 1.0 / float(np.sqrt(128.0))
        for blk in range(NBLK):
            v = 0 if blk < 4 else 1
            dma("pool", Hr, q_s[blk].rearrange("p (c t) -> p c t", c=NCH), reads=[("q", blk, m) for m in range(16)],
                writes=rk("H"))
            if blk < 4:
                qsegs = [(0, TB, list(range(16)) + [20, 21, 22, 23])]
            else:
                qsegs = [(0, 256, [16, 17]), (256, 512, [18, 19])]
            for g in range(4):
                kview = wslot(0)[:, 0:T]
                vview = wslot(1)[:, 0:20 * 128].rearrange("p (c d) -> p c d", c=20)
                kreads = [("k", b, g) for b in range(NBLK)] + [("kc", tq) for tq in range(4)]
                vreads = [("v", b) for b in range(6)]
                if blk < 4:
                    dma("pool", kview[:, 0:TS].rearrange("p (c t) -> p c t", t=512), kT_s[g][:, 0:TS].rearrange("p (c t) -> p c t", t=512), reads=kreads, writes=[("w", 0)])
                    dma("pool", kview[:, TS:T], kT_s[g][:, T:TK], reads=kreads, writes=[("w", 0)])
                    dma("pool", vview[:, 0:16, :],
                        v_s[0:TS, g * 128:(g + 1) * 128].rearrange("(c p) d -> p c d", p=128), reads=vreads,
                        writes=[("w", 1)])
                    dma("pool", vview[:, 16:20, :],
                        v_s[T:TK, g * 128:(g + 1) * 128].rearrange("(c p) d -> p c d", p=128), reads=vreads,
                        writes=[("w", 1)])
                    kcol = lambda kc: kc * 128 if kc < 16 else TS + (kc - 20) * 128
                    vidx = lambda kc: kc if kc < 16 else 16 + (kc - 20)
                else:
                    dma("pool", kview[:, 0:TB], kT_s[g][:, TS:T], reads=kreads, writes=[("w", 0)])
                    dma("pool", vview[:, 0:4, :],
                        v_s[TS:T, g * 128:(g + 1) * 128].rearrange("(c p) d -> p c d", p=128), reads=vreads,
                        writes=[("w", 1)])
                    kcol = lambda kc: (kc - 16) * 128
                    vidx = lambda kc: kc - 16
                items = []
                for hh in range(4):
                    for (q0, q1, kcs) in qsegs:
                        for i, kc in enumerate(kcs):
                            items.append((g * 4 + hh, q0, q1, i, len(kcs), kc))

                def score(it):
                    hd, q0, q1, i, nkc, kc = it
                    pss, pks = newps(lo=True)
                    mm(pss[:, q0:q1], kview[:, kcol(kc):kcol(kc) + 128], Hr[:, hd, q0:q1], True, True,
                       reads=[("w", 0)] + rk("H", hd), writes=[pks])
                    pi = pTR.next()
                    act(pT_t[pi][:, q0:q1], pss[:, q0:q1], AF.Exp, reads=[pks], writes=[("pT", pi)],
                        scale=scale)
                    return pi
                acc = None
                pcur = score(items[0])
                for idx, it in enumerate(items):
                    hd, q0, q1, i, nkc, kc = it
                    pnext = score(items[idx + 1]) if idx + 1 < len(items) else None
                    if i == 0:
                        bo, bd = accR.next()
                        acc = (ps_t[bo], ("ps", bo), ps_t[bd], ("ps", bd))
                    pso, pko, psd, pkd = acc
                    mm(pso[:, q0:q1], vview[:, vidx(kc), :], pT_t[pcur][:, q0:q1], i == 0, i == nkc - 1,
                       reads=[("w", 1), ("pT", pcur)], writes=[pko])
                    mm(psd[:, q0:q1], ones[:], pT_t[pcur][:, q0:q1], i == 0, i == nkc - 1,
                       reads=["ones", ("pT", pcur)], writes=[pkd])
                    if i == nkc - 1:
                        ri = rstdR.next()
                        A("dve", lambda h, o=rstd_t[ri][:, q0:q1], i_=psd[:, q0:q1]: h.reciprocal(o, i_),
                          reads=[pkd], writes=[("rstd", ri)])
                        tt(Hr[:, hd, q0:q1], pso[:, q0:q1], rstd_t[ri][:, q0:q1], ALU.mult,
                           reads=[pko, ("rstd", ri)], writes=rk("H", hd))
                    pcur = pnext
            def evac(m, ps, pk):
                act(Ov[:, m, :], ps[:], AF.Copy, reads=[pk], writes=rk("O", m))
            linear("H", w_o[jl], 0, 16, evac)
            load_x(blk)
            post_norm_res(l, v, 0, "O")
            ffn(l, v)
            store_x(blk)
        S.barrier()

    for l in range(depth):
        if l % 2 == 0:
            rg_layer(l)
            mod_drain(1000)
        else:
            attn_layer(l)

    for blk in range(NBLK):
        load_x(blk)
        for tq in range(4):
            for q4 in range(4):
                ps, pk = newps()
                for cc in range(4):
                    c = q4 * 4 + cc
                    transp(ps[:, cc * 128:(cc + 1) * 128], Xv[:, c, tq * 128:(tq + 1) * 128], ident[:],
                           reads=rk("X", c) + ["ident"], writes=[pk])
                dst = OT[:, tq * 2048 + q4 * 512: tq * 2048 + (q4 + 1) * 512]
                act(dst, ps[:], AF.Copy, reads=[pk], writes=[("O", 4 * tq + q4)])
            r0 = blk * TB + tq * 128
            dma("sp", y_tok[r0:r0 + 128, :], OT[:, tq * 2048: (tq + 1) * 2048],
                reads=[("O", 4 * tq + i) for i in range(4)], writes=[("yout", blk, tq)])
    ps, pk = newps()
    transp(ps[:, 0:128], stout[:], ident[:], reads=["stout", "ident"], writes=[pk])
    ti = tmpR.next()
    act(tmp_t[ti][:, 0:128], ps[:, 0:128], AF.Copy, reads=[pk], writes=[("tmp", ti)])
    dma("sp", nstate, tmp_t[ti][:, 0:128], reads=[("tmp", ti)], writes=["nstate"])

    S.emit(nc, block, sems, dsems)
    es.close()
    return nc


def make_consts():
    c = np.zeros((128, 3 * 128 + 2 * TS), np.float32)
    c[:, 0:128] = np.eye(128, dtype=np.float32)
    c[:, 128:256] = 1.0
    P = np.zeros((128, 128), np.float32)
    for half in range(2):
        b = half * 64
        for i in range(32):
            P[b + 32 + i, b + i] = -1.0
            P[b + i, b + 32 + i] = 1.0
    c[:, 256:384] = P
    t = np.arange(TS)
    row = (t // 64).astype(np.float32)
    col = (t % 64).astype(np.float32)
    inv_freq = (np.float32(10000.0) ** (-np.arange(32, dtype=np.float32) / np.float32(32))).astype(np.float32)
    ang_r = (row[:, None] * inv_freq[None, :]).astype(np.float32)
    ang_c = (col[:, None] * inv_freq[None, :]).astype(np.float32)
    cosT = np.zeros((128, TS), np.float32)
    sinT = np.zeros((128, TS), np.float32)
    for d in range(128):
        ang = ang_r if d < 64 else ang_c
        i = d % 32
        cosT[d] = np.cos(ang[:, i])
        sinT[d] = np.sin(ang[:, i])
    c[:, 384:384 + TS] = cosT
    c[:, 384 + TS:] = sinT
    return c


_NC_CACHE = {}


def kernel(x_prompt, x_sample, state_rglru, cache_k, cache_v, c, c_ctx,
           w_mod, b_mod, g_pre_mix, g_post_mix, g_pre_ffn, g_post_ffn,
           w_qkv, g_q, g_k, w_o,
           w_rg_in, rg_conv_w, rg_conv_b, w_rg_a, b_rg_a, w_rg_x, b_rg_x, rg_lambda, w_rg_out,
           w_ff1, w_ff2, _depth=L):
    f = lambda a: np.ascontiguousarray(np.asarray(a, dtype=np.float32))
    x_prompt, x_sample = f(x_prompt), f(x_sample)
    if _depth not in _NC_CACHE:
        _NC_CACHE[_depth] = build(_depth)
    nc = _NC_CACHE[_depth]
    consts = make_consts()
    shared = {
        "consts": consts, "w_mod": f(w_mod), "w_qkv": f(w_qkv), "w_o": f(w_o), "w_rg_in": f(w_rg_in),
        "w_rg_a": f(w_rg_a), "w_rg_x": f(w_rg_x), "w_rg_out": f(w_rg_out), "w_ff1": f(w_ff1), "w_ff2": f(w_ff2),
    }
    in_maps = []
    for core in range(8):
        b = core % 2
        small = np.zeros((NSMALL, 128), np.float32)
        small[OFF_BMOD:OFF_BMOD + 384] = f(b_mod).reshape(384, 128)
        small[OFF_GPM:OFF_GPM + 64] = f(g_pre_mix).reshape(64, 128)
        small[OFF_GQM:OFF_GQM + 64] = f(g_post_mix).reshape(64, 128)
        small[OFF_GPF:OFF_GPF + 64] = f(g_pre_ffn).reshape(64, 128)
        small[OFF_GQF:OFF_GQF + 64] = f(g_post_ffn).reshape(64, 128)
        small[OFF_CW:OFF_CW + 128] = f(rg_conv_w).reshape(128, 128)
        small[OFF_CB:OFF_CB + 32] = f(rg_conv_b).reshape(32, 128)
        small[OFF_BA:OFF_BA + 64] = f(b_rg_a).reshape(64, 128)
        small[OFF_BX:OFF_BX + 64] = f(b_rg_x).reshape(64, 128)
        small[OFF_LAM:OFF_LAM + 64] = f(rg_lambda).reshape(64, 128)
        small[OFF_ST:OFF_ST + 64] = f(state_rglru)[b].reshape(64, 128)
        small[OFF_CVEC:OFF_CVEC + 16] = f(c)[b].reshape(16, 128)
        small[OFF_CVEC + 16:OFF_CVEC + 32] = f(c_ctx).reshape(16, 128)
        small[OFF_GQ:OFF_GQ + 2] = f(g_q)
        small[OFF_GK:OFF_GK + 2] = f(g_k)
        x_tok = np.concatenate([x_sample[b], x_prompt[2 * core:2 * core + 2].reshape(512, D)], axis=0)
        m = dict(shared)
        m["x_tok"] = np.ascontiguousarray(x_tok)
        m["small"] = small
        m["ck"] = f(cache_k)[b].reshape(2, PAST, 512)
        m["cv"] = f(cache_v)[b].reshape(2, PAST, 512)
        in_maps.append(m)
    res = run_bass_kernel_spmd(nc, in_maps, core_ids=list(range(8)))
    r = res.results
    y_prompt = np.zeros((16, 256, D), np.float32)
    y_sample = np.zeros((2, TS, D), np.float32)
    nst = np.zeros((16, 2, 2, D), np.float32)
    nkk = np.zeros((16, 2, 256, 4, 128), np.float32)
    nvv = np.zeros((16, 2, 256, 4, 128), np.float32)
    for core in range(8):
        yt = np.asarray(r[core]["y_tok"])
        y_prompt[2 * core:2 * core + 2] = yt[TS:].reshape(2, 256, D)
        if core < 2:
            y_sample[core] = yt[:TS]
        nst[2 * core:2 * core + 2] = np.asarray(r[core]["nstate"]).reshape(2, 2, 2, D)
        nkk[2 * core:2 * core + 2] = np.asarray(r[core]["nk"]).reshape(2, 2, 256, 4, 128)
        nvv[2 * core:2 * core + 2] = np.asarray(r[core]["nv"]).reshape(2, 2, 256, 4, 128)
    return (y_prompt, y_sample, nst, nkk, nvv)
```

```python
import numpy as np
import concourse.bass as bass
import concourse.mybir as mybir
from concourse.bass_utils import run_bass_kernel_spmd

F32 = mybir.dt.float32
F32R = mybir.dt.float32r
BF16 = mybir.dt.bfloat16
MMT = BF16
AF = mybir.ActivationFunctionType
ALU = mybir.AluOpType

D = 2048
NCH = 16
L = 4
TB = 512
NBLK = 5
T = 2560
TS = 2048
PAST = 512
TK = T + PAST
DFF = 8192
EPS = 1e-6
QKVW = 3072
SEGS = [(0, 2048), (2048, 2304), (2304, 2560)]

OFF_BMOD = 0
OFF_GPM = 384
OFF_GQM = 448
OFF_GPF = 512
OFF_GQF = 576
OFF_CW = 640
OFF_CB = 768
OFF_BA = 800
OFF_BX = 864
OFF_LAM = 928
OFF_ST = 992
OFF_CVEC = 1056
OFF_GQ = 1088
OFF_GK = 1090
NSMALL = 1152

ENGS = ("pe", "act", "dve", "pool", "sp")
NDCH = 8


class Op:
    __slots__ = ("eng", "fn", "deps", "sig", "val", "dma", "sem", "prev")


class Sched:
    def __init__(self):
        self.ops = {e: [] for e in ENGS}
        self.keys = {}
        self.last = {e: None for e in ENGS}
        self.fence = {e: [] for e in ENGS}
        self.recent_dma = {e: [] for e in ENGS}

    def add(self, eng, fn, reads=(), writes=(), dma=False):
        op = Op()
        op.eng = eng
        op.fn = fn
        op.dma = dma
        op.sig = dma
        op.val = 0
        op.sem = None
        op.prev = None
        deps = []
        for k in reads:
            st = self.keys.get(k)
            if st is not None and st[0] is not None:
                deps.append(st[0])
        for k in writes:
            st = self.keys.get(k)
            if st is not None:
                if st[0] is not None:
                    deps.append(st[0])
                deps.extend(st[1])
        deps.extend(self.fence[eng])
        self.fence[eng] = []
        seen = set()
        od = []
        for d in deps:
            if id(d) in seen:
                continue
            seen.add(id(d))
            if d.eng == "pe" and eng == "pe" and not d.dma:
                continue
            od.append(d)
        op.deps = od
        for d in od:
            d.sig = True
        for k in reads:
            st = self.keys.get(k)
            if st is None:
                self.keys[k] = [None, [op]]
            else:
                st[1].append(op)
        for k in writes:
            self.keys[k] = [op, []]
        self.ops[eng].append(op)
        self.last[eng] = op
        if dma:
            self.recent_dma[eng] = (self.recent_dma[eng] + [op])[-NDCH:]
        return op

    def barrier(self):
        lasts = [o for o in self.last.values() if o is not None]
        for e in ENGS:
            lasts.extend(self.recent_dma[e])
        for e in ENGS:
            self.fence[e] = list(lasts)

    def emit(self, nc, block, sems, dsems):
        for e in ENGS:
            v = 0
            chc = [0] * NDCH
            chlast = [None] * NDCH
            n = 0
            for op in self.ops[e]:
                if op.dma:
                    ch = n % NDCH
                    n += 1
                    chc[ch] += 1
                    op.sem = dsems[e][ch]
                    op.val = 16 * chc[ch]
                    op.prev = chlast[ch]
                    chlast[ch] = op
                elif op.sig:
                    v += 1
                    op.val = v
                    op.sem = sems[e]
        finals = []
        for e in ENGS:
            for op in self.ops[e]:
                if op.dma:
                    finals.append(op)
        fin = {}
        for op in finals:
            fin[id(op.sem)] = (op.sem, max(op.val, fin.get(id(op.sem), (None, 0))[1]))

        def run(h, e):
            seen = {}
            for op in self.ops[e]:
                waits = [(d.sem, d.val) for d in op.deps]
                if op.dma and op.prev is not None:
                    waits.append((op.prev.sem, op.prev.val))
                for sem, val in waits:
                    if seen.get(id(sem), 0) < val:
                        h.wait_ge(sem, val)
                        seen[id(sem)] = val
                ins = op.fn(h)
                if op.dma:
                    ins.then_inc(op.sem, 16)
                elif op.sig:
                    ins.then_inc(op.sem, 1)
            if e == "sp":
                for sem, val in fin.values():
                    if seen.get(id(sem), 0) < val:
                        h.wait_ge(sem, val)

        @block.tensor
        def _(h):
            run(h, "pe")

        @block.scalar
        def _(h):
            run(h, "act")

        @block.vector
        def _(h):
            run(h, "dve")

        @block.gpsimd
        def _(h):
            run(h, "pool")

        @block.sync
        def _(h):
            run(h, "sp")


class Ring:
    def __init__(self, items):
        self.items = items
        self.i = 0

    def next(self):
        it = self.items[self.i % len(self.items)]
        self.i += 1
        return it


def build(depth=L):
    nc = bass.Bass("TRN2", target_bir_lowering=False)
    S = Sched()

    def din(name, shape):
        return nc.dram_tensor(name, list(shape), F32, kind="ExternalInput").ap()

    def dout(name, shape):
        return nc.dram_tensor(name, list(shape), F32, kind="ExternalOutput").ap()

    def dscr(name, shape):
        return nc.dram_tensor(name, list(shape), F32).ap()

    x_tok = din("x_tok", [T, D])
    small = din("small", [NSMALL, 128])
    ck = din("ck", [2, PAST, 512])
    cv = din("cv", [2, PAST, 512])
    consts = din("consts", [128, 3 * 128 + 2 * TS])
    w_mod = din("w_mod", [L, D, 6 * D])
    w_qkv = din("w_qkv", [2, D, QKVW])
    w_o = din("w_o", [2, D, D])
    w_rg_in = din("w_rg_in", [2, D, 2 * D])
    w_rg_a = din("w_rg_a", [2, 2, 16, 128, 128])
    w_rg_x = din("w_rg_x", [2, 2, 16, 128, 128])
    w_rg_out = din("w_rg_out", [2, D, D])
    w_ff1 = din("w_ff1", [L, D, DFF])
    w_ff2 = din("w_ff2", [L, DFF, D])

    y_tok = dout("y_tok", [T, D])
    nstate = dout("nstate", [128, 128])
    nk = dout("nk", [2, 2, 256, 512])
    nv = dout("nv", [2, 2, 256, 512])

    xres = dscr("xres", [NBLK, 128, NCH * TB])
    gate_s = dscr("gate_s", [NCH, 128, T])
    xb_s = dscr("xb_s", [NCH, 128, T])
    y_s = dscr("y_s", [NCH, 128, T])
    q_s = dscr("q_s", [NBLK, 128, NCH * TB])
    kT_s = dscr("kT_s", [4, 128, TK])
    v_s = dscr("v_s", [TK, 512])

    from contextlib import ExitStack
    es = ExitStack()

    def sb(name, shape, dt=F32):
        return es.enter_context(nc.sbuf_tensor(name, list(shape), dt))

    wring = sb("wring", [128, 6 * 4096], MMT)
    XT = sb("XT", [128, 8192], F32)
    HT = sb("HT", [128, 8192], MMT)
    XC = sb("XC", [128, T], F32)
    TB1 = sb("TB1", [128, T], F32)
    TB2 = sb("TB2", [128, T], F32)
    OT = sb("OT", [128, 8192], F32)
    smallT = sb("smallT", [128, NSMALL])
    modT = sb("modT", [128, L * 2 * 96])
    dm = sb("dm", [128, L * 2 * 4 * 16])
    spT = sb("spT", [128, 128])
    sl = sb("sl", [128, 16, 2], MMT)
    sl32 = sb("sl32", [128, 16, 2], F32)
    ident = sb("ident", [128, 128])
    ones = sb("ones", [128, 128], MMT)
    prot = sb("prot", [128, 128], MMT)
    cs = sb("cs", [128, 2, TB])
    stout = sb("stout", [128, 128])
    cst = sb("cst", [128, 4])
    sq_t = [sb("sq%d" % i, [128, TB], MMT) for i in range(4)]
    tmp_t = [sb("tmp%d" % i, [128, TB]) for i in range(3)]
    rstd_t = [sb("rstd%d" % i, [128, TB]) for i in range(2)]
    hid_t = [sb("hid%d" % i, [128, 2, TB], MMT) for i in range(2)]
    pT_t = [sb("pT%d" % i, [128, TB], MMT) for i in range(3)]
    gw_t = [sb("gw%d" % i, [128, 4, 128], MMT) for i in range(2)]
    ps_t = [es.enter_context(nc.psum_tensor("ps%d" % i, [128, TB], F32)) for i in range(8)]
    sems = {e: es.enter_context(nc.semaphore("s_" + e)) for e in ENGS}
    dsems = {e: [es.enter_context(nc.semaphore("d_%s%d" % (e, i))) for i in range(NDCH)] for e in ENGS}
    block = es.enter_context(nc.Block())

    tmp_t = tmp_t + [TB1[:, i * TB:(i + 1) * TB] for i in range(5)]
    rstd_t = rstd_t + [TB2[:, i * TB:(i + 1) * TB] for i in range(4)]
    sqR = Ring(list(range(4)))
    tmpR = Ring(list(range(8)))
    rstdR = Ring(list(range(6)))
    hidR = Ring(list(range(2)))
    pTR = Ring(list(range(3)))
    gwR = Ring(list(range(2)))
    psR = Ring(list(range(8)))
    psLo = Ring([0, 1, 2, 3])
    accR = Ring([(4, 5), (6, 7)])
    wR = Ring(list(range(6)))

    Xv = XT[:].rearrange("p (c t) -> p c t", c=NCH)
    Ov = OT[:].rearrange("p (c t) -> p c t", c=NCH)
    Hr = HT[:].rearrange("p (c t) -> p c t", c=NCH)
    REG = {"X": Xv, "O": Ov}
    REGR = {"H": Hr}

    def rk(reg, c=None):
        if c is None:
            return [(reg, i) for i in range(NCH)]
        return [(reg, c)]

    def wslot(i):
        return wring[:, i * 4096:(i + 1) * 4096]

    A = S.add

    def newps(lo=False):
        i = psLo.next() if lo else psR.next()
        return ps_t[i], ("ps", i)

    def dma(eng, out, in_, reads, writes):
        if eng == "pool":
            A(eng, lambda h, o=out, i=in_: h.dma_start(out=o, in_=i, max_dma_last_dim=8192), reads=reads,
              writes=writes, dma=True)
        else:
            A(eng, lambda h, o=out, i=in_: h.dma_start(out=o, in_=i), reads=reads, writes=writes, dma=True)

    def act(out, in_, func, reads, writes, bias=None, scale=1.0):
        def f(h, o=out, i=in_, fn=func, b=bias, s=scale):
            if b is None:
                return h.activation(out=o, in_=i, func=fn, scale=s)
            return h.activation(out=o, in_=i, func=fn, bias=b, scale=s)
        A("act", f, reads=reads, writes=writes)

    def tt(out, in0, in1, op, reads, writes):
        A("dve", lambda h, o=out, a=in0, b=in1, p=op: h.tensor_tensor(o, a, b, p), reads=reads, writes=writes)

    def stt(out, in0, scalar, in1, op0, op1, reads, writes):
        A("dve", lambda h, o=out, a=in0, s=scalar, b=in1, p0=op0, p1=op1:
          h.scalar_tensor_tensor(o, a, s, b, p0, p1), reads=reads, writes=writes)

    def ts(out, in0, s1, s2, op0, op1, reads, writes):
        if s2 is None:
            A("dve", lambda h, o=out, a=in0, x=s1, p0=op0: h.tensor_scalar(o, a, x, None, p0),
              reads=reads, writes=writes)
        else:
            A("dve", lambda h, o=out, a=in0, x=s1, y=s2, p0=op0, p1=op1: h.tensor_scalar(o, a, x, y, p0, p1),
              reads=reads, writes=writes)

    def mm(ps, lhsT, rhs, start, stop, reads, writes):
        A("pe", lambda h, o=ps, l=lhsT, r=rhs, s0=start, s1=stop: h.matmul(o, l, r, start=s0, stop=s1),
          reads=reads, writes=writes)

    def mmgroup(ps, pairs, reads, writes):
        def f(h, o=ps, pr=pairs):
            n = len(pr)
            ins = None
            for i, (l, r) in enumerate(pr):
                ins = h.matmul(o, l, r, start=(i == 0), stop=(i == n - 1))
            return ins
        A("pe", f, reads=reads, writes=writes)

    def transp(ps, in_, idn, reads, writes):
        A("pe", lambda h, o=ps, i=in_, d=idn: h.transpose(o, i, d), reads=reads, writes=writes)

    onec = cst[:, 0:1]
    epsc = cst[:, 1:2]

    def rstd_from(srcs, inv_n):
        W = srcs[0][0].shape[-1]
        ps, pk = newps()
        n = len(srcs)
        for c, (ap, keys) in enumerate(srcs):
            si = sqR.next()
            act(sq_t[si][:, 0:W], ap, AF.Square, reads=keys, writes=[("sq", si)])
            mm(ps[:, 0:W], ones[:], sq_t[si][:, 0:W], c == 0, c == n - 1, reads=[("sq", si), "ones"], writes=[pk])
        ti = tmpR.next()
        act(tmp_t[ti][:, 0:W], ps[:, 0:W], AF.Sqrt, reads=[pk, "cst"], writes=[("tmp", ti)], bias=epsc, scale=inv_n)
        ri = rstdR.next()
        A("dve", lambda h, o=rstd_t[ri][:, 0:W], i=tmp_t[ti][:, 0:W]: h.reciprocal(o, i),
          reads=[("tmp", ti)], writes=[("rstd", ri)])
        return ri

    def dmcol(l, v, which, c):
        base = ((l * 2 + v) * 4 + which) * 16 + c
        return dm[:, base:base + 1]

    def modcol(l, v, m, c):
        base = (l * 2 + v) * 96 + m * 16 + c
        return modT[:, base:base + 1]

    def norm_mod(l, v, sub):
        ri = rstd_from([(Xv[:, c, :], rk("X", c)) for c in range(NCH)], 1.0 / D)
        for c in range(NCH):
            ti = tmpR.next()
            stt(tmp_t[ti][:], Xv[:, c, :], dmcol(l, v, 2 * sub, c), rstd_t[ri][:], ALU.mult, ALU.mult,
                reads=rk("X", c) + [("rstd", ri), ("dm", l)], writes=[("tmp", ti)])
            act(Hr[:, c, :], tmp_t[ti][:], AF.Identity, reads=[("tmp", ti), ("modT", l)], writes=rk("H", c),
                bias=modcol(l, v, 3 * sub, c))

    def post_norm_res(l, v, sub, oreg):
        ov = REG[oreg]
        ri = rstd_from([(ov[:, c, :], rk(oreg, c)) for c in range(NCH)], 1.0 / D)
        for c in range(NCH):
            ti = tmpR.next()
            stt(tmp_t[ti][:], ov[:, c, :], dmcol(l, v, 2 * sub + 1, c), rstd_t[ri][:], ALU.mult, ALU.mult,
                reads=rk(oreg, c) + [("rstd", ri), ("dm", l)], writes=[("tmp", ti)])
            tt(Xv[:, c, :], Xv[:, c, :], tmp_t[ti][:], ALU.add, reads=rk("X", c) + [("tmp", ti)], writes=rk("X", c))

    def load_w(dram_ap, shape3):
        wi = wR.next()
        a, b = shape3
        view = wslot(wi)[:, 0:a * b].rearrange("p (a b) -> p a b", a=a)
        if b > 512:
            dma("pool", wslot(wi)[:, 0:a * b].rearrange("p (a c d) -> p a c d", a=a, d=512),
                dram_ap.rearrange("p a (c d) -> p a c d", d=512), reads=[], writes=[("w", wi)])
        else:
            dma("pool", view, dram_ap, reads=[], writes=[("w", wi)])
        return view, ("w", wi)

    def linear(inreg, wmat, col0, nchunks, evac):
        hin = REGR[inreg]
        wv = wmat.rearrange("(kc p) n -> p kc n", p=128)
        for t in range(nchunks // 2):
            view, wk = load_w(wv[:, :, col0 + t * 256: col0 + (t + 1) * 256], (16, 256))
            for j in range(2):
                ps, pk = newps()
                mmgroup(ps[:], [(view[:, k, j * 128:(j + 1) * 128], hin[:, k, :]) for k in range(NCH)],
                        reads=[wk] + rk(inreg), writes=[pk])
                evac(2 * t + j, ps, pk)

    def ffn(l, v):
        norm_mod(l, v, 1)
        w1v = w_ff1[l].rearrange("(kc p) n -> p kc n", p=128)
        w2v = w_ff2[l].rearrange("(g kc p) n -> g p kc n", kc=2, p=128)
        def ffn2(g, hi):
            view2, wk2 = load_w(w2v[g], (2, 2048))
            for n in range(NCH):
                ps, pk = newps()
                mmgroup(ps[:], [(view2[:, kc, n * 128:(n + 1) * 128], hid_t[hi][:, kc, :]) for kc in range(2)],
                        reads=[wk2, ("hid", hi, 0), ("hid", hi, 1)], writes=[pk])
                if g == 0:
                    act(Ov[:, n, :], ps[:], AF.Copy, reads=[pk], writes=rk("O", n))
                else:
                    tt(Ov[:, n, :], Ov[:, n, :], ps[:], ALU.add, reads=[pk] + rk("O", n), writes=rk("O", n))
        prev = None
        for g in range(DFF // 256):
            view, wk = load_w(w1v[:, :, g * 256:(g + 1) * 256], (16, 256))
            hi = hidR.next()
            for j in range(2):
                ps, pk = newps()
                mmgroup(ps[:], [(view[:, k, j * 128:(j + 1) * 128], Hr[:, k, :]) for k in range(NCH)],
                        reads=[wk] + rk("H"), writes=[pk])
                ti = tmpR.next()
                act(tmp_t[ti][:], ps[:], AF.Relu, reads=[pk], writes=[("tmp", ti)])
                tt(hid_t[hi][:, j, :], tmp_t[ti][:], tmp_t[ti][:], ALU.mult, reads=[("tmp", ti)],
                   writes=[("hid", hi, j)])
            if prev is not None:
                ffn2(*prev)
            prev = (g, hi)
        ffn2(*prev)
        post_norm_res(l, v, 1, "O")

    def load_x(blk):
        dma("sp", XT[:], xres[blk], reads=[("xres", blk)], writes=rk("X"))

    def store_x(blk):
        dma("sp", xres[blk], XT[:], reads=rk("X"), writes=[("xres", blk)])

    A("dve", lambda h: h.memset(cst[:, 0:1], 1.0), writes=["cst0"])
    A("dve", lambda h: h.memset(cst[:, 1:2], EPS), reads=["cst0"], writes=["cst"])
    dma("sp", ident[:], consts[:, 0:128], reads=[], writes=["ident"])
    dma("pool", ones[:], consts[:, 128:256], reads=[], writes=["ones"])
    dma("pool", prot[:], consts[:, 256:384], reads=[], writes=["prot"])
    for i in range(NSMALL // 128):
        ti = tmpR.next()
        dma("sp", tmp_t[ti][:, 0:128], small[i * 128:(i + 1) * 128, :], reads=[], writes=[("tmp", ti)])
        ps, pk = newps()
        transp(ps[:, 0:128], tmp_t[ti][:, 0:128], ident[:], reads=[("tmp", ti), "ident"], writes=[pk])
        act(smallT[:, i * 128:(i + 1) * 128], ps[:, 0:128], AF.Copy, reads=[pk], writes=["smallT"])
    for v in range(2):
        act(sl[:, :, v], smallT[:, OFF_CVEC + v * 16: OFF_CVEC + (v + 1) * 16], AF.Silu, reads=["smallT"],
            writes=[("sl", v)])
        act(sl32[:, :, v], smallT[:, OFF_CVEC + v * 16: OFF_CVEC + (v + 1) * 16], AF.Silu, reads=["smallT"],
            writes=[("sl32", v)])
    act(spT[:, 0:64], smallT[:, OFF_LAM:OFF_LAM + 64], AF.Exp, reads=["smallT"], writes=["spT"], scale=-1.0)
    act(spT[:, 0:64], spT[:, 0:64], AF.Ln, reads=["spT", "cst"], writes=["spT"], bias=onec)
    ts(spT[:, 64:128], spT[:, 0:64], -16.0, None, ALU.mult, None, reads=["spT"], writes=["spT2"])
    ts(spT[:, 0:64], spT[:, 0:64], -8.0, None, ALU.mult, None, reads=["spT", "spT2"], writes=["spT"])
    def mod_unit(l, t):
        wv = w_mod[l].rearrange("(kc p) n -> p kc n", p=128)
        view, wk = load_w(wv[:, :, t * 256:(t + 1) * 256], (16, 256))
        for j in range(2):
            cc = 2 * t + j
            ps, pk = newps()
            mmgroup(ps[:, 0:2], [(view[:, k, j * 128:(j + 1) * 128], sl[:, k, :]) for k in range(NCH)],
                    reads=[wk, ("sl", 0), ("sl", 1)], writes=[pk])
            mview = modT[:, l * 192: (l + 1) * 192].rearrange("p (v c) -> p v c", v=2)[:, :, cc]
            bcol = OFF_BMOD + l * 96 + cc
            ts(mview, ps[:, 0:2], smallT[:, bcol:bcol + 1], None, ALU.add, None,
               reads=[pk, "smallT"], writes=[("modT", l)])

    def mod_derive(l):
        for v in range(2):
            for sub in range(2):
                gpre = OFF_GPM if sub == 0 else OFF_GPF
                gpost = OFF_GQM if sub == 0 else OFF_GQF
                mbase = (l * 2 + v) * 96 + sub * 48
                dbase = ((l * 2 + v) * 4 + 2 * sub) * 16
                stt(dm[:, dbase:dbase + 16], modT[:, mbase + 16: mbase + 32], 1.0,
                    smallT[:, gpre + l * 16: gpre + (l + 1) * 16], ALU.add, ALU.mult,
                    reads=[("modT", l), "smallT"], writes=[("dm", l)])
                tt(dm[:, dbase + 16:dbase + 32], modT[:, mbase + 32: mbase + 48],
                   smallT[:, gpost + l * 16: gpost + (l + 1) * 16], ALU.mult,
                   reads=[("modT", l), "smallT", ("dm", l)], writes=[("dm", l)])

    fR = Ring([0, 1])
    modcnt = [0]

    def mod_unit_hw(l, t):
        usep = (modcnt[0] % 10) in (1, 3, 6, 8)
        modcnt[0] += 1
        wv = w_mod[l].rearrange("(kc p) n -> p kc n", p=128)
        p = 2 if usep else fR.next()
        stage = wring[:, p * 8192:(p + 1) * 8192].bitcast(F32)
        skeys = [("w", 2 * p), ("w", 2 * p + 1)]
        dma("sp", stage.rearrange("p (a b) -> p a b", a=16), wv[:, :, t * 256:(t + 1) * 256], reads=[],
            writes=skeys)
        if usep:
            wb = HT[:, 4096:8192]
            A("pool", lambda h, o=wb, i=stage: h.tensor_copy(o, i), reads=skeys, writes=["HTmod"])
            view = wb.rearrange("p (a b) -> p a b", a=16)
            rhs = sl
            rkeys = ["HTmod", ("sl", 0), ("sl", 1)]
        else:
            view = stage.rearrange("p (a b) -> p a b", a=16)
            rhs = sl32
            rkeys = skeys + [("sl32", 0), ("sl32", 1)]
        for j in range(2):
            cc = 2 * t + j
            ps, pk = newps()
            mmgroup(ps[:, 0:2], [(view[:, k, j * 128:(j + 1) * 128], rhs[:, k, :]) for k in range(NCH)],
                    reads=rkeys, writes=[pk])
            mview = modT[:, l * 192: (l + 1) * 192].rearrange("p (v c) -> p v c", v=2)[:, :, cc]
            bcol = OFF_BMOD + l * 96 + cc
            ts(mview, ps[:, 0:2], smallT[:, bcol:bcol + 1], None, ALU.add, None,
               reads=[pk, "smallT"], writes=[("modT", l)])

    mod_pending = []
    for l in range(depth):
        if l == 0:
            for t in range(48):
                mod_unit(l, t)
            mod_derive(l)
        else:
            mod_pending.extend([(l, t) for t in range(48)] + [(l, None)])

    def mod_drain(nunits):
        for _ in range(nunits):
            if not mod_pending:
                return
            l, t = mod_pending.pop(0)
            if t is None:
                mod_derive(l)
            else:
                mod_unit_hw(l, t)

    for blk in range(NBLK):
        for tq in range(4):
            r0 = blk * TB + tq * 128
            dma("sp", OT[:, tq * 2048: (tq + 1) * 2048], x_tok[r0:r0 + 128, :], reads=[],
                writes=[("O", 4 * tq + i) for i in range(4)])
        for c in range(NCH):
            ps, pk = newps()
            for tq in range(4):
                src = OT[:, tq * 2048 + c * 128: tq * 2048 + (c + 1) * 128]
                transp(ps[:, tq * 128:(tq + 1) * 128], src, ident[:], reads=rk("O") + ["ident"], writes=[pk])
            act(Xv[:, c, :], ps[:], AF.Copy, reads=[pk], writes=rk("X", c))
        store_x(blk)

    def rg_layer(l):
        jl = l // 2
        for blk in range(NBLK):
            v = 0 if blk < 4 else 1
            load_x(blk)
            norm_mod(l, v, 0)

            def evac(m, ps, pk, blk=blk):
                ti = tmpR.next()
                act(tmp_t[ti][:], ps[:], AF.Copy, reads=[pk], writes=[("tmp", ti)])
                dst = gate_s if m < 16 else xb_s
                n = m % 16
                dma("sp", dst[n][:, blk * TB:(blk + 1) * TB], tmp_t[ti][:], reads=[("tmp", ti)],
                    writes=[("gx", m, blk)])
            linear("H", w_rg_in[jl], 0, 32, evac)
        S.barrier()
        Tt = [XT[:, i * T:(i + 1) * T] for i in range(3)] + [OT[:, i * T:(i + 1) * T] for i in range(3)]
        Tt.append(XC[:])
        T7r = HT[:, 0:T]
        for n in range(NCH):
            xb, xc = Tt[5], Tt[6]
            dma("pool", xb, xb_s[n], reads=[("gx", 16 + n, b) for b in range(NBLK)], writes=["T5"])
            gi = gwR.next()
            dma("pool", gw_t[gi][:, 0:2, :], w_rg_a[jl, :, n].rearrange("e j k -> j e k"), reads=[],
                writes=[("gwa", gi)])
            dma("pool", gw_t[gi][:, 2:4, :], w_rg_x[jl, :, n].rearrange("e j k -> j e k"), reads=[],
                writes=[("gwx", gi)])
            cwc = lambda k: smallT[:, OFF_CW + jl * 64 + k * 16 + n: OFF_CW + jl * 64 + k * 16 + n + 1]
            cbc = smallT[:, OFF_CB + jl * 16 + n: OFF_CB + jl * 16 + n + 1]
            for (s0, s1) in SEGS:
                ts(xc[:, s0:s1], xb[:, s0:s1], cwc(1), cbc, ALU.mult, ALU.add, reads=["T5", "smallT"],
                   writes=["T6"])
                for k in (0, 2, 3):
                    o = k - 1
                    a0 = max(s0, s0 - o)
                    a1 = min(s1, s1 - o)
                    stt(xc[:, a0:a1], xb[:, a0 + o:a1 + o], cwc(k), xc[:, a0:a1], ALU.mult, ALU.add,
                        reads=["T5", "T6", "smallT"], writes=["T6"])
            act(T7r, xc, AF.Copy, reads=["T6"], writes=["T6r"])
            deferred = []
            for e in range(2):
                t1, t2 = (Tt[0], Tt[1]) if e == 0 else (TB1[:], TB2[:])
                k1, k2 = ("T0", "T1") if e == 0 else ("TB1", "TB2")
                t3 = Tt[2] if e == 0 else Tt[3]
                k3 = "T2" if e == 0 else "T3"
                bacol = OFF_BA + (jl * 2 + e) * 16 + n
                bxcol = OFF_BX + (jl * 2 + e) * 16 + n
                spcol = (jl * 2 + e) * 16 + n
                for blk in range(NBLK):
                    c0, c1 = blk * TB, (blk + 1) * TB
                    ps, pk = newps()
                    mm(ps[:], gw_t[gi][:, e, :], T7r[:, c0:c1], True, True, reads=[("gwa", gi), "T6r"], writes=[pk])
                    act(t1[:, c0:c1], ps[:], AF.Sigmoid, reads=[pk, "smallT"], writes=[k1],
                        bias=smallT[:, bacol:bacol + 1])
                    ps, pk = newps()
                    mm(ps[:], gw_t[gi][:, 2 + e, :], T7r[:, c0:c1], True, True, reads=[("gwx", gi), "T6r"],
                       writes=[pk])
                    act(t2[:, c0:c1], ps[:], AF.Sigmoid, reads=[pk, "smallT"], writes=[k2],
                        bias=smallT[:, bxcol:bxcol + 1])
                act(t3, t1, AF.Exp, reads=[k1, "spT2"], writes=[k3], scale=spT[:, 64 + spcol: 64 + spcol + 1])
                act(t3, t3, AF.Sqrt, reads=[k3, "cst"], writes=[k3], bias=onec, scale=-1.0)
                act(t1, t1, AF.Exp, reads=[k1, "spT"], writes=[k1], scale=spT[:, spcol:spcol + 1])
                tt(t2, t2, t3, ALU.mult, reads=[k2, k3], writes=[k2])
                tt(t2, t2, xc, ALU.mult, reads=[k2, "T6"], writes=[k2])
                for si, (s0, s1) in enumerate(SEGS):
                    if si == 0:
                        init = smallT[:, OFF_ST + (jl * 2 + e) * 16 + n: OFF_ST + (jl * 2 + e) * 16 + n + 1]
                    else:
                        init = 0.0
                    if e == 0:
                        o_, a_, u_ = t3[:, s0:s1], t1[:, s0:s1], t2[:, s0:s1]
                    else:
                        o_, a_, u_ = t3[:, s0:s1][:, ::-1], t1[:, s0:s1][:, ::-1], t2[:, s0:s1][:, ::-1]
                    A("dve", lambda h, o=o_, a=a_, u=u_, i0=init: h.tensor_tensor_scan(o, a, u, i0, ALU.mult, ALU.add),
                      reads=[k1, k2, "smallT"], writes=[k3])
                    if si > 0:
                        col = (((si - 1) * 2 + jl) * 2 + e) * 16 + n
                        pos = s1 - 1 if e == 0 else s0
                        deferred.append((stout[:, col:col + 1], t3[:, pos:pos + 1], k3))
            dma("pool", Tt[4], gate_s[n], reads=[("gx", n, b) for b in range(NBLK)], writes=["T4"])
            act(Tt[4], Tt[4], AF.Gelu_apprx_tanh, reads=["T4"], writes=["T4"])
            for (o_, i_, kk) in deferred:
                act(o_, i_, AF.Copy, reads=[kk], writes=["stout"])
            tt(Tt[2], Tt[2], Tt[3], ALU.add, reads=["T2", "T3"], writes=["T2"])
            tt(Tt[2], Tt[2], Tt[4], ALU.mult, reads=["T2", "T4"], writes=["T2"])
            dma("pool", y_s[n], Tt[2], reads=["T2"], writes=[("y", n)])
            mod_drain(10)
        S.barrier()
        for blk in range(NBLK):
            v = 0 if blk < 4 else 1
            dma("pool", Hr, y_s[:, :, blk * TB:(blk + 1) * TB].rearrange("c p t -> p c t"),
                reads=[("y", n) for n in range(NCH)], writes=rk("H"))
            load_x(blk)

            def evac(m, ps, pk):
                act(Ov[:, m, :], ps[:], AF.Copy, reads=[pk], writes=rk("O", m))
            linear("H", w_rg_out[jl], 0, 16, evac)
            post_norm_res(l, v, 0, "O")
            ffn(l, v)
            store_x(blk)
        S.barrier()

    def attn_layer(l):
        jl = l // 2
        dma("sp", v_s[T:TK, :], cv[jl], reads=[], writes=[("v", 5)])
        for tq in range(4):
            ti = tmpR.next()
            dma("sp", tmp_t[ti][:], ck[jl, tq * 128:(tq + 1) * 128, :], reads=[], writes=[("tmp", ti)])
            ps, pk = newps()
            for g in range(4):
                transp(ps[:, g * 128:(g + 1) * 128], tmp_t[ti][:, g * 128:(g + 1) * 128], ident[:],
                       reads=[("tmp", ti), "ident"], writes=[pk])
            t2i = tmpR.next()
            act(tmp_t[t2i][:], ps[:], AF.Copy, reads=[pk], writes=[("tmp", t2i)])
            dma("sp", kT_s[:, :, T + tq * 128: T + (tq + 1) * 128].rearrange("g p t -> p g t"),
                tmp_t[t2i][:].rearrange("p (g t) -> p g t", g=4), reads=[("tmp", t2i)], writes=[("kc", tq)])
        for blk in range(NBLK):
            v = 0 if blk < 4 else 1
            load_x(blk)
            norm_mod(l, v, 0)
            if blk < 4:
                dma("sp", cs[:, 0, :], consts[:, 384 + blk * TB: 384 + (blk + 1) * TB], reads=[], writes=["cs0"])
                dma("sp", cs[:, 1, :], consts[:, 384 + TS + blk * TB: 384 + TS + (blk + 1) * TB], reads=[],
                    writes=["cs1"])

            def evac_a(m, ps, pk, blk=blk):
                isq = m < 16
                gcol = (OFF_GQ if isq else OFF_GK) + jl
                ri = rstd_from([(ps[:], [pk])], 1.0 / 128)
                ti = tmpR.next()
                si = None
                stt(tmp_t[ti][:], ps[:], smallT[:, gcol:gcol + 1], rstd_t[ri][:], ALU.mult, ALU.mult,
                    reads=[pk, ("rstd", ri), "smallT"], writes=[("tmp", ti)])
                if blk < 4:
                    si = sqR.next()
                    act(sq_t[si][:], tmp_t[ti][:], AF.Copy, reads=[("tmp", ti)], writes=[("sq", si)])
                return (m, ti, si)

            def evac_b(st, blk=blk):
                m, ti, si = st
                isq = m < 16
                if blk < 4:
                    ps2, pk2 = newps()
                    mm(ps2[:], prot[:], sq_t[si][:], True, True, reads=[("sq", si), "prot"], writes=[pk2])
                    t2i = tmpR.next()
                    tt(tmp_t[t2i][:], ps2[:], cs[:, 1, :], ALU.mult, reads=[pk2, "cs1"], writes=[("tmp", t2i)])
                    tt(tmp_t[ti][:], tmp_t[ti][:], cs[:, 0, :], ALU.mult, reads=[("tmp", ti), "cs0"],
                       writes=[("tmp", ti)])
                    tt(tmp_t[ti][:], tmp_t[ti][:], tmp_t[t2i][:], ALU.add, reads=[("tmp", ti), ("tmp", t2i)],
                       writes=[("tmp", ti)])
                if isq:
                    dma("sp", q_s[blk][:, m * TB:(m + 1) * TB], tmp_t[ti][:], reads=[("tmp", ti)],
                        writes=[("q", blk, m)])
                else:
                    g = m - 16
                    dma("sp", kT_s[g][:, blk * TB:(blk + 1) * TB], tmp_t[ti][:], reads=[("tmp", ti)],
                        writes=[("k", blk, g)])
                    if blk == 4:
                        ps3, pk3 = newps()
                        for tq in range(4):
                            transp(ps3[:, tq * 128:(tq + 1) * 128], tmp_t[ti][:, tq * 128:(tq + 1) * 128], ident[:],
                                   reads=[("tmp", ti), "ident"], writes=[pk3])
                        t3i = tmpR.next()
                        act(tmp_t[t3i][:], ps3[:], AF.Copy, reads=[pk3], writes=[("tmp", t3i)])
                        for sg in range(2):
                            dma("sp", nk[sg, jl][:, g * 128:(g + 1) * 128].rearrange("(a p) d -> p a d", p=128),
                                tmp_t[t3i][:, sg * 256:(sg + 1) * 256].rearrange("p (a d) -> p a d", a=2),
                                reads=[("tmp", t3i)], writes=[("nk", jl, g, sg)])

            pend_a, pend_b = [], []
            wvq = w_qkv[jl].rearrange("(kc p) n -> p kc n", p=128)
            for t in range(10):
                view, wk = load_w(wvq[:, :, t * 256:(t + 1) * 256], (16, 256))
                for j in range(2):
                    ps, pk = newps()
                    mmgroup(ps[:], [(view[:, k, j * 128:(j + 1) * 128], Hr[:, k, :]) for k in range(NCH)],
                            reads=[wk] + rk("H"), writes=[pk])
                    pend_a.append((2 * t + j, ps, pk))
                    if len(pend_a) > 1:
                        pend_b.append(evac_a(*pend_a.pop(0)))
                    if len(pend_b) > 1:
                        evac_b(pend_b.pop(0))
            while pend_a:
                pend_b.append(evac_a(*pend_a.pop(0)))
            while pend_b:
                evac_b(pend_b.pop(0))
            wv = w_qkv[jl].rearrange("(kc p) n -> p kc n", p=128)
            vh = []
            for half in range(2):
                view, wk = load_w(wv[:, :, 2560 + half * 256: 2560 + (half + 1) * 256], (16, 256))
                vh.append((view, wk))
            for tq in range(4):
                ps, pk = newps()
                for half in range(2):
                    view, wk = vh[half]
                    mmgroup(ps[:, half * 256:(half + 1) * 256],
                            [(Hr[:, k, tq * 128:(tq + 1) * 128], view[:, k, :]) for k in range(NCH)],
                            reads=[wk] + rk("H"), writes=[pk])
                ti = tmpR.next()
                act(tmp_t[ti][:], ps[:], AF.Copy, reads=[pk], writes=[("tmp", ti)])
                r0 = blk * TB + tq * 128
                dma("sp", v_s[r0:r0 + 128, :], tmp_t[ti][:], reads=[("tmp", ti)], writes=[("v", blk)])
                if blk == 4:
                    dma("sp", nv[tq // 2, jl, (tq % 2) * 128:(tq % 2 + 1) * 128, :], tmp_t[ti][:],
                        reads=[("tmp", ti)], writes=[("nv", jl, tq)])
        S.barrier()
        scale = 1.0 / float(np.sqrt(128.0))
        for blk in range(NBLK):
            v = 0 if blk < 4 else 1
            dma("pool", Hr, q_s[blk].rearrange("p (c t) -> p c t", c=NCH), reads=[("q", blk, m) for m in range(16)],
                writes=rk("H"))
            if blk < 4:
                qsegs = [(0, TB, list(range(16)) + [20, 21, 22, 23])]
            else:
                qsegs = [(0, 256, [16, 17]), (256, 512, [18, 19])]
            for g in range(4):
                kview = wslot(0)[:, 0:T]
                vview = wslot(1)[:, 0:20 * 128].rearrange("p (c d) -> p c d", c=20)
                kreads = [("k", b, g) for b in range(NBLK)] + [("kc", tq) for tq in range(4)]
                vreads = [("v", b) for b in range(6)]
                if blk < 4:
                    dma("pool", kview[:, 0:TS].rearrange("p (c t) -> p c t", t=512), kT_s[g][:, 0:TS].rearrange("p (c t) -> p c t", t=512), reads=kreads, writes=[("w", 0)])
                    dma("pool", kview[:, TS:T], kT_s[g][:, T:TK], reads=kreads, writes=[("w", 0)])
                    dma("pool", vview[:, 0:16, :],
                        v_s[0:TS, g * 128:(g + 1) * 128].rearrange("(c p) d -> p c d", p=128), reads=vreads,
                        writes=[("w", 1)])
                    dma("pool", vview[:, 16:20, :],
                        v_s[T:TK, g * 128:(g + 1) * 128].rearrange("(c p) d -> p c d", p=128), reads=vreads,
                        writes=[("w", 1)])
                    kcol = lambda kc: kc * 128 if kc < 16 else TS + (kc - 20) * 128
                    vidx = lambda kc: kc if kc < 16 else 16 + (kc - 20)
                else:
                    dma("pool", kview[:, 0:TB], kT_s[g][:, TS:T], reads=kreads, writes=[("w", 0)])
                    dma("pool", vview[:, 0:4, :],
                        v_s[TS:T, g * 128:(g + 1) * 128].rearrange("(c p) d -> p c d", p=128), reads=vreads,
                        writes=[("w", 1)])
                    kcol = lambda kc: (kc - 16) * 128
                    vidx = lambda kc: kc - 16
                items = []
                for hh in range(4):
                    for (q0, q1, kcs) in qsegs:
                        for i, kc in enumerate(kcs):
                            items.append((g * 4 + hh, q0, q1, i, len(kcs), kc))

                def score(it):
                    hd, q0, q1, i, nkc, kc = it
                    pss, pks = newps(lo=True)
                    mm(pss[:, q0:q1], kview[:, kcol(kc):kcol(kc) + 128], Hr[:, hd, q0:q1], True, True,
                       reads=[("w", 0)] + rk("H", hd), writes=[pks])
                    pi = pTR.next()
                    act(pT_t[pi][:, q0:q1], pss[:, q0:q1], AF.Exp, reads=[pks], writes=[("pT", pi)],
                        scale=scale)
                    return pi
                acc = None
                pcur = score(items[0])
                for idx, it in enumerate(items):
                    hd, q0, q1, i, nkc, kc = it
                    pnext = score(items[idx + 1]) if idx + 1 < len(items) else None
                    if i == 0:
                        bo, bd = accR.next()
                        acc = (ps_t[bo], ("ps", bo), ps_t[bd], ("ps", bd))
                    pso, pko, psd, pkd = acc
                    mm(pso[:, q0:q1], vview[:, vidx(kc), :], pT_t[pcur][:, q0:q1], i == 0, i == nkc - 1,
                       reads=[("w", 1), ("pT", pcur)], writes=[pko])
                    mm(psd[:, q0:q1], ones[:], pT_t[pcur][:, q0:q1], i == 0, i == nkc - 1,
                       reads=["ones", ("pT", pcur)], writes=[pkd])
                    if i == nkc - 1:
                        ri = rstdR.next()
                        A("dve", lambda h, o=rstd_t[ri][:, q0:q1], i_=psd[:, q0:q1]: h.reciprocal(o, i_),
                          reads=[pkd], writes=[("rstd", ri)])
                        tt(Hr[:, hd, q0:q1], pso[:, q0:q1], rstd_t[ri][:, q0:q1], ALU.mult,
                           reads=[pko, ("rstd", ri)], writes=rk("H", hd))
                    pcur = pnext
            def evac(m, ps, pk):
                act(Ov[:, m, :], ps[:], AF.Copy, reads=[pk], writes=rk("O", m))
            linear("H", w_o[jl], 0, 16, evac)
            load_x(blk)
            post_norm_res(l, v, 0, "O")
            ffn(l, v)
            store_x(blk)
        S.barrier()

    for l in range(depth):
        if l % 2 == 0:
            rg_layer(l)
            mod_drain(1000)
        else:
            attn_layer(l)

    for blk in range(NBLK):
        load_x(blk)
        for tq in range(4):
            for q4 in range(4):
                ps, pk = newps()
                for cc in range(4):
                    c = q4 * 4 + cc
                    transp(ps[:, cc * 128:(cc + 1) * 128], Xv[:, c, tq * 128:(tq + 1) * 128], ident[:],
                           reads=rk("X", c) + ["ident"], writes=[pk])
                dst = OT[:, tq * 2048 + q4 * 512: tq * 2048 + (q4 + 1) * 512]
                act(dst, ps[:], AF.Copy, reads=[pk], writes=[("O", 4 * tq + q4)])
            r0 = blk * TB + tq * 128
            dma("sp", y_tok[r0:r0 + 128, :], OT[:, tq * 2048: (tq + 1) * 2048],
                reads=[("O", 4 * tq + i) for i in range(4)], writes=[("yout", blk, tq)])
    ps, pk = newps()
    transp(ps[:, 0:128], stout[:], ident[:], reads=["stout", "ident"], writes=[pk])
    ti = tmpR.next()
    act(tmp_t[ti][:, 0:128], ps[:, 0:128], AF.Copy, reads=[pk], writes=[("tmp", ti)])
    dma("sp", nstate, tmp_t[ti][:, 0:128], reads=[("tmp", ti)], writes=["nstate"])

    S.emit(nc, block, sems, dsems)
    es.close()
    return nc


def make_consts():
    c = np.zeros((128, 3 * 128 + 2 * TS), np.float32)
    c[:, 0:128] = np.eye(128, dtype=np.float32)
    c[:, 128:256] = 1.0
    P = np.zeros((128, 128), np.float32)
    for half in range(2):
        b = half * 64
        for i in range(32):
            P[b + 32 + i, b + i] = -1.0
            P[b + i, b + 32 + i] = 1.0
    c[:, 256:384] = P
    t = np.arange(TS)
    row = (t // 64).astype(np.float32)
    col = (t % 64).astype(np.float32)
    inv_freq = (np.float32(10000.0) ** (-np.arange(32, dtype=np.float32) / np.float32(32))).astype(np.float32)
    ang_r = (row[:, None] * inv_freq[None, :]).astype(np.float32)
    ang_c = (col[:, None] * inv_freq[None, :]).astype(np.float32)
    cosT = np.zeros((128, TS), np.float32)
    sinT = np.zeros((128, TS), np.float32)
    for d in range(128):
        ang = ang_r if d < 64 else ang_c
        i = d % 32
        cosT[d] = np.cos(ang[:, i])
        sinT[d] = np.sin(ang[:, i])
    c[:, 384:384 + TS] = cosT
    c[:, 384 + TS:] = sinT
    return c


_NC_CACHE = {}


def kernel(x_prompt, x_sample, state_rglru, cache_k, cache_v, c, c_ctx,
           w_mod, b_mod, g_pre_mix, g_post_mix, g_pre_ffn, g_post_ffn,
           w_qkv, g_q, g_k, w_o,
           w_rg_in, rg_conv_w, rg_conv_b, w_rg_a, b_rg_a, w_rg_x, b_rg_x, rg_lambda, w_rg_out,
           w_ff1, w_ff2, _depth=L):
    f = lambda a: np.ascontiguousarray(np.asarray(a, dtype=np.float32))
    x_prompt, x_sample = f(x_prompt), f(x_sample)
    if _depth not in _NC_CACHE:
        _NC_CACHE[_depth] = build(_depth)
    nc = _NC_CACHE[_depth]
    consts = make_consts()
    shared = {
        "consts": consts, "w_mod": f(w_mod), "w_qkv": f(w_qkv), "w_o": f(w_o), "w_rg_in": f(w_rg_in),
        "w_rg_a": f(w_rg_a), "w_rg_x": f(w_rg_x), "w_rg_out": f(w_rg_out), "w_ff1": f(w_ff1), "w_ff2": f(w_ff2),
    }
    in_maps = []
    for core in range(8):
        b = core % 2
        small = np.zeros((NSMALL, 128), np.float32)
        small[OFF_BMOD:OFF_BMOD + 384] = f(b_mod).reshape(384, 128)
        small[OFF_GPM:OFF_GPM + 64] = f(g_pre_mix).reshape(64, 128)
        small[OFF_GQM:OFF_GQM + 64] = f(g_post_mix).reshape(64, 128)
        small[OFF_GPF:OFF_GPF + 64] = f(g_pre_ffn).reshape(64, 128)
        small[OFF_GQF:OFF_GQF + 64] = f(g_post_ffn).reshape(64, 128)
        small[OFF_CW:OFF_CW + 128] = f(rg_conv_w).reshape(128, 128)
        small[OFF_CB:OFF_CB + 32] = f(rg_conv_b).reshape(32, 128)
        small[OFF_BA:OFF_BA + 64] = f(b_rg_a).reshape(64, 128)
        small[OFF_BX:OFF_BX + 64] = f(b_rg_x).reshape(64, 128)
        small[OFF_LAM:OFF_LAM + 64] = f(rg_lambda).reshape(64, 128)
        small[OFF_ST:OFF_ST + 64] = f(state_rglru)[b].reshape(64, 128)
        small[OFF_CVEC:OFF_CVEC + 16] = f(c)[b].reshape(16, 128)
        small[OFF_CVEC + 16:OFF_CVEC + 32] = f(c_ctx).reshape(16, 128)
        small[OFF_GQ:OFF_GQ + 2] = f(g_q)
        small[OFF_GK:OFF_GK + 2] = f(g_k)
        x_tok = np.concatenate([x_sample[b], x_prompt[2 * core:2 * core + 2].reshape(512, D)], axis=0)
        m = dict(shared)
        m["x_tok"] = np.ascontiguousarray(x_tok)
        m["small"] = small
        m["ck"] = f(cache_k)[b].reshape(2, PAST, 512)
        m["cv"] = f(cache_v)[b].reshape(2, PAST, 512)
        in_maps.append(m)
    res = run_bass_kernel_spmd(nc, in_maps, core_ids=list(range(8)))
    r = res.results
    y_prompt = np.zeros((16, 256, D), np.float32)
    y_sample = np.zeros((2, TS, D), np.float32)
    nst = np.zeros((16, 2, 2, D), np.float32)
    nkk = np.zeros((16, 2, 256, 4, 128), np.float32)
    nvv = np.zeros((16, 2, 256, 4, 128), np.float32)
    for core in range(8):
        yt = np.asarray(r[core]["y_tok"])
        y_prompt[2 * core:2 * core + 2] = yt[TS:].reshape(2, 256, D)
        if core < 2:
            y_sample[core] = yt[:TS]
        nst[2 * core:2 * core + 2] = np.asarray(r[core]["nstate"]).reshape(2, 2, 2, D)
        nkk[2 * core:2 * core + 2] = np.asarray(r[core]["nk"]).reshape(2, 2, 256, 4, 128)
        nvv[2 * core:2 * core + 2] = np.asarray(r[core]["nv"]).reshape(2, 2, 256, 4, 128)
    return (y_prompt, y_sample, nst, nkk, nvv)
```
